# Optimizing a Trainium2 kernel written in Bass

```python
import math
import jax, jax.numpy as jnp
from jax import lax
import numpy as np

D_MODEL = 4096
BATCH = 4
SEQ = 4096
DEPTH = 1

HEAD_DIM = 128
N_Q_HEADS = 16
N_KV_HEADS = 4
GQA_GROUP = N_Q_HEADS // N_KV_HEADS
ATTN_W = N_Q_HEADS * HEAD_DIM
KV_W = N_KV_HEADS * HEAD_DIM
WINDOW = 128
BLOCK = 128
N_BUCKETS = 32
MAX_DISTANCE = 128
GMLP_HEADS = 16
GMLP_HEAD_DIM = 128
GMLP_W = GMLP_HEADS * GMLP_HEAD_DIM
CHUNK = 128
MIX_W = ATTN_W + GMLP_W
IN_W = ATTN_W + 2 * KV_W + 2 * GMLP_W
D_FF = 4 * D_MODEL
EPS = 1e-6
NEG = -1e30

kernel_name = "hybrid_window_gqa_gmlp_block"


def rmsnorm(x, g):
    xf = x.astype(jnp.float32)
    y = xf * lax.rsqrt(jnp.mean(xf * xf, axis=-1, keepdims=True) + EPS)
    return (y * g.astype(jnp.float32)).astype(x.dtype)


def t5_bucket(rel):
    nb = N_BUCKETS // 2
    max_exact = nb // 2
    ret = (rel > 0).astype(np.int32) * nb
    n = np.abs(rel)
    large = max_exact + (np.log(np.maximum(n, 1).astype(np.float32) / max_exact)
                         / math.log(MAX_DISTANCE / max_exact) * (nb - max_exact)).astype(np.int32)
    large = np.minimum(large, nb - 1)
    return ret + np.where(n < max_exact, n, large)


def band_layout(seq):
    nblk = seq // BLOCK
    a = np.arange(BLOCK)[:, None]
    s = np.arange(3 * BLOCK)[None, :]
    rel = s - BLOCK - a
    blk = np.arange(nblk)[:, None, None]
    k_pos = (blk - 1) * BLOCK + s[None]
    valid = (k_pos >= 0) & (k_pos < seq) & (np.abs(rel)[None] <= WINDOW)
    return rel, valid


def banded(t, nblk):
    B, _, H, D = t.shape
    tp = jnp.pad(t, ((0, 0), (BLOCK, BLOCK), (0, 0), (0, 0)))
    tb = tp.reshape(B, nblk + 2, BLOCK, H, D)
    return jnp.concatenate([tb[:, :-2], tb[:, 1:-1], tb[:, 2:]], axis=2)


def windowed_gqa(q, k, v, q_gain, k_gain, rel_bias, sink):
    B, S, _ = q.shape
    nblk = S // BLOCK
    q = rmsnorm(q.reshape(B, S, N_Q_HEADS, HEAD_DIM), q_gain)
    k = rmsnorm(k.reshape(B, S, N_KV_HEADS, HEAD_DIM), k_gain)
    v = v.reshape(B, S, N_KV_HEADS, HEAD_DIM)
    qb = q.reshape(B, nblk, BLOCK, N_KV_HEADS, GQA_GROUP, HEAD_DIM)
    kb = banded(k, nblk)
    vb = banded(v, nblk)
    rel, valid = band_layout(S)
    bias = rel_bias[jnp.asarray(t5_bucket(rel))]
    bias = jnp.transpose(bias, (2, 0, 1)).reshape(N_KV_HEADS, GQA_GROUP, BLOCK, 3 * BLOCK)
    s = jnp.einsum('bnqkgd,bnskd->bnkgqs', qb, kb).astype(jnp.float32) * (HEAD_DIM ** -0.5)
    s = jnp.where(jnp.asarray(valid)[None, :, None, None], s + bias.astype(jnp.float32), NEG)
    sink_l = sink.astype(jnp.float32).reshape(N_KV_HEADS, GQA_GROUP)[:, :, None, None]
    m = jnp.maximum(jnp.max(s, axis=-1, keepdims=True), sink_l)
    p = jnp.exp(s - m)
    p = p / (jnp.sum(p, axis=-1, keepdims=True) + jnp.exp(sink_l - m))
    o = jnp.einsum('bnkgqs,bnskd->bnqkgd', p.astype(v.dtype), vb)
    return o.reshape(B, S, ATTN_W)


def spatial_gating(u, v, v_gain, w_s, b_s):
    B, S, _ = u.shape
    nchunk = S // CHUNK
    u = jax.nn.gelu(u)
    v = rmsnorm(jax.nn.gelu(v), v_gain)
    vc = v.reshape(B, nchunk, CHUNK, GMLP_HEADS, GMLP_HEAD_DIM)
    sv = jnp.einsum('hts,bcshd->bcthd', w_s, vc) + b_s.T[None, None, :, :, None]
    return u * sv.reshape(B, S, GMLP_W)


def setup_inputs(seed: int = 0) -> dict:
    key = jax.random.key(seed)
    ks = jax.random.split(key, 16)
    f = jnp.float32
    L = DEPTH
    nrm = lambda k, shape, sc: jax.random.normal(k, shape, f) * sc
    return {
        "x": nrm(ks[0], (BATCH, SEQ, D_MODEL), 1.0),
        "norm1": 1.0 + nrm(ks[1], (L, D_MODEL), 0.02),
        "w_in": nrm(ks[2], (L, D_MODEL, IN_W), D_MODEL ** -0.5),
        "q_gain": 1.0 + nrm(ks[3], (L, HEAD_DIM), 0.02),
        "k_gain": 1.0 + nrm(ks[4], (L, HEAD_DIM), 0.02),
        "rel_bias": nrm(ks[5], (N_BUCKETS, N_Q_HEADS), 0.5),
        "attn_sink": nrm(ks[6], (L, N_Q_HEADS), 0.5),
        "attn_out_gain": 1.0 + nrm(ks[7], (L, ATTN_W), 0.02),
        "gmlp_v_gain": 1.0 + nrm(ks[8], (L, GMLP_W), 0.02),
        "gmlp_w_s": nrm(ks[9], (L, GMLP_HEADS, CHUNK, CHUNK), CHUNK ** -0.5),
        "gmlp_b_s": 1.0 + nrm(ks[10], (L, GMLP_HEADS, CHUNK), 0.1),
        "gmlp_out_gain": 1.0 + nrm(ks[11], (L, GMLP_W), 0.02),
        "w_out": nrm(ks[12], (L, MIX_W, D_MODEL), MIX_W ** -0.5),
        "norm2": 1.0 + nrm(ks[13], (L, D_MODEL), 0.02),
        "w1": nrm(ks[14], (L, D_MODEL, D_FF), D_MODEL ** -0.5),
        "w2": nrm(ks[15], (L, D_FF, D_MODEL), D_FF ** -0.5),
    }


def reference(x, norm1, w_in, q_gain, k_gain, rel_bias, attn_sink, attn_out_gain,
              gmlp_v_gain, gmlp_w_s, gmlp_b_s, gmlp_out_gain, w_out, norm2, w1, w2):
    o_k = ATTN_W
    o_v = o_k + KV_W
    o_u = o_v + KV_W
    o_g = o_u + GMLP_W
    for l in range(DEPTH):
        h = rmsnorm(x, norm1[l])
        z = jnp.einsum('bsd,de->bse', h, w_in[l])
        a = windowed_gqa(z[..., :o_k], z[..., o_k:o_v], z[..., o_v:o_u],
                         q_gain[l], k_gain[l], rel_bias, attn_sink[l])
        g = spatial_gating(z[..., o_u:o_g], z[..., o_g:], gmlp_v_gain[l],
                           gmlp_w_s[l], gmlp_b_s[l])
        mix = jnp.concatenate([rmsnorm(a, attn_out_gain[l]), rmsnorm(g, gmlp_out_gain[l])], axis=-1)
        x = x + jnp.einsum('bse,ed->bsd', mix, w_out[l])
        h = rmsnorm(x, norm2[l])
        hid = jnp.square(jax.nn.relu(jnp.einsum('bsd,df->bsf', h, w1[l])))
        x = x + jnp.einsum('bsf,fd->bsd', hid, w2[l])
    return x
```

```python
import math
import os
from functools import reduce

import numpy as np
import concourse.bass as bass
import concourse.mybir as mybir
from concourse.bass_utils import run_bass_kernel_spmd

F32 = mybir.dt.float32
BF16 = mybir.dt.bfloat16
U8 = mybir.dt.uint8
AF = mybir.ActivationFunctionType
ALU = mybir.AluOpType
AX = mybir.AxisListType
DTSZ = {F32: 4, BF16: 2, U8: 1}

D = 4096
NCORE = 8
TOK_CORE = 2048
NTILE = 4
TB = 4
TT = 512
IN_W = 7168
D_FF = 16384
EPS = 1e-6
O_K, O_V, O_U, O_G = 2048, 2560, 3072, 5120

OFF_HT = 0
OFF_MIXT = 32768
OFF_X1 = 65536
OFF_SLOTS = 131072
NSLOT = 6
SLOT_B = 8192
OFF_MISC = OFF_SLOTS + NSLOT * SLOT_B
M_HBF = OFF_MISC
M_TMPA = M_HBF + 8192
M_WST = M_TMPA + 8192
M_SMALL = M_WST + 4096
M_SAVE = M_SMALL + 30 * 256
ARENA = M_SAVE + 4096
A_E = OFF_X1 + 0
A_QT = OFF_X1 + 24576
A_KT = OFF_X1 + 40960
A_V = OFF_X1 + 47104
A_PT = OFF_X1 + 53440
A_GV = OFF_X1 + 0
A_GB = OFF_X1 + 16384
A_VG = OFF_X1 + 32768
B_XS0 = OFF_MIXT
B_HALO = OFF_MIXT + 16384
B_AP = OFF_MIXT + 16384


def _prod(s):
    return reduce(lambda a, b: a * b, s, 1)


class Op:
    __slots__ = ("eng", "fn", "is_dma", "chan", "deps", "signal", "tick", "idx", "dval")

    def __init__(self, eng, fn, is_dma, chan):
        self.eng = eng
        self.fn = fn
        self.is_dma = is_dma
        self.chan = chan
        self.deps = {}
        self.signal = False
        self.tick = 0
        self.dval = 0


class Cell:
    __slots__ = ("w", "dw", "r", "dr")

    def __init__(self):
        self.w = {}
        self.dw = []
        self.r = {}
        self.dr = []


class Sched:
    ENGS = ("pe", "act", "dve", "pool", "sp")

    def __init__(self):
        self.ops = {e: [] for e in self.ENGS}
        self.cells = {}
        self.chan_last = {}
        self.chan_count = {}
        self.nops = 0

    @staticmethod
    def _cells(ap):
        name = ap.tensor.name
        if name == "arena":
            cs = 256
        elif name == "psum":
            cs = 2048
        else:
            return ()
        esz = DTSZ[ap.dtype]
        pstep = tuple(ap.ap)[0][0]
        off = int(ap.offset)
        if pstep > 0:
            off = off % pstep
        lo = off * esz
        hi = lo + esz
        for step, cnt in tuple(ap.ap)[1:]:
            if step >= 0:
                hi += step * (cnt - 1) * esz
            else:
                lo += step * (cnt - 1) * esz
        return [(name, c) for c in range(lo // cs, (hi - 1) // cs + 1)]

    def add(self, eng, fn, reads=(), writes=(), chan=None, after=()):
        is_dma = chan is not None
        op = Op(eng, fn, is_dma, chan)
        deps = op.deps

        def dep(o, raw):
            if o is op:
                return
            if o.is_dma:
                deps[o] = True
                return
            same = (o.eng == eng)
            if same and not is_dma and eng == "pe":
                return
            deps[o] = True

        rcells = []
        for ap in reads:
            rcells.extend(self._cells(ap))
        wcells = []
        for ap in writes:
            wcells.extend(self._cells(ap))
        for key in rcells:
            st = self.cells.get(key)
            if st is None:
                st = self.cells[key] = Cell()
            for o in st.w.values():
                dep(o, True)
            for o in st.dw:
                dep(o, True)
        for key in wcells:
            st = self.cells.get(key)
            if st is None:
                st = self.cells[key] = Cell()
            for o in st.r.values():
                dep(o, False)
            for o in st.dr:
                dep(o, False)
            for o in st.w.values():
                dep(o, False)
            for o in st.dw:
                dep(o, False)
        for o in after:
            dep(o, True)
        if is_dma:
            prev = self.chan_last.get(chan)
            if prev is not None:
                deps[prev] = True
            self.chan_last[chan] = op
            n = self.chan_count.get(chan, 0) + 1
            self.chan_count[chan] = n
            op.dval = 16 * n
        for o in deps:
            if not o.is_dma:
                o.signal = True
        for key in rcells:
            st = self.cells[key]
            if is_dma:
                st.dr.append(op)
            else:
                st.r[eng] = op
        for key in wcells:
            st = self.cells[key]
            if st.r or st.dr:
                st.r = {}
                st.dr = []
                st.dw = []
            if is_dma:
                st.dw.append(op)
            else:
                st.w[eng] = op
        self.ops[eng].append(op)
        self.nops += 1
        return op

    def emit(self, nc, block, sems, chan_sems):
        for e in self.ENGS:
            t = 0
            for op in self.ops[e]:
                if op.signal and not op.is_dma:
                    t += 1
                    op.tick = t

        def run(eng_name, engine):
            waited = {}
            for op in self.ops[eng_name]:
                need = {}
                for o in op.deps:
                    if o.is_dma:
                        s, v = chan_sems[o.chan], o.dval
                    else:
                        s, v = sems[o.eng], o.tick
                    if waited.get(s, 0) >= v:
                        continue
                    if need.get(s, (None, 0))[1] < v:
                        need[s] = (s, v)
                for s, v in need.values():
                    engine.wait_ge(s, v)
                    waited[s] = v
                if op.fn is None:
                    continue
                ins = op.fn(engine)
                if op.is_dma:
                    ins.then_inc(chan_sems[op.chan], 16)
                elif op.signal:
                    ins.then_inc(sems[op.eng], 1)

        block.tensor(lambda e: run("pe", e))
        block.scalar(lambda e: run("act", e))
        block.vector(lambda e: run("dve", e))
        block.gpsimd(lambda e: run("pool", e))
        block.sync(lambda e: run("sp", e))


def build_program(debug=None):
    nc = bass.Bass("TRN2", target_bir_lowering=False)
    xp = nc.dram_tensor("xp", [TOK_CORE + 256, D], F32, kind="ExternalInput").ap()
    edge_d = nc.dram_tensor("edge", [128, 2], F32, kind="ExternalInput").ap()
    tb_d = nc.dram_tensor("tb", [16, 512], F32, kind="ExternalInput").ap()
    gt_d = nc.dram_tensor("gt", [128, 128], F32, kind="ExternalInput").ap()
    valid_d = nc.dram_tensor("valid", [16, 512], F32, kind="ExternalInput").ap()
    w_in_d = nc.dram_tensor("w_in", [D, IN_W], F32, kind="ExternalInput").ap()
    sink_d = nc.dram_tensor("attn_sink", [1, 16], F32, kind="ExternalInput").ap()
    vgain_d = nc.dram_tensor("gmlp_v_gain", [1, 2048], F32, kind="ExternalInput").ap()
    ws_d = nc.dram_tensor("gmlp_w_s", [16, 128, 128], F32, kind="ExternalInput").ap()
    w_out_d = nc.dram_tensor("w_out", [D, D], F32, kind="ExternalInput").ap()
    w1_d = nc.dram_tensor("w1", [D, D_FF], F32, kind="ExternalInput").ap()
    w2_d = nc.dram_tensor("w2", [D_FF, D], F32, kind="ExternalInput").ap()
    y_d = nc.dram_tensor("y", [TOK_CORE, D], F32, kind="ExternalOutput").ap()
    tsc_t = nc.dram_tensor("tsc", [16, 512], F32)
    tsc = tsc_t.ap()
    dbg_d = None
    if debug:
        dbg_d = nc.dram_tensor("dbg", [128, 16384], F32, kind="ExternalOutput").ap()

    arena = nc.alloc_sbuf_tensor("arena", [128, ARENA], U8)
    psum = nc.alloc_psum_tensor("psum", [128, 4096], F32)

    def V(off, dt, *shape):
        n = _prod(shape) * DTSZ[dt]
        ap = arena[:, off:off + n].bitcast(dt)
        if len(shape) == 2:
            ap = ap.rearrange("p (a b) -> p a b", b=shape[1])
        elif len(shape) == 3:
            ap = ap.rearrange("p (a b c) -> p a b c", b=shape[1], c=shape[2])
        return ap

    hT = V(OFF_HT, BF16, 32, TT)
    mixT = V(OFF_MIXT, BF16, 32, TT)
    x1 = V(OFF_X1, F32, TB, D)
    slots = [OFF_SLOTS + i * SLOT_B for i in range(NSLOT)]
    hbf = V(M_HBF, BF16, D)
    tmpA = V(M_TMPA, F32, 2048)
    wsT = V(M_WST, BF16, 16, 128)
    sm = [M_SMALL + i * 256 for i in range(30)]
    ident = V(sm[0], BF16, 128)
    Jm = V(sm[1], BF16, 128)
    GT = V(sm[27], F32, 128)
    g1T = GT[:, 0:32]
    g2T = GT[:, 32:64]
    aogT = GT[:, 64:80]
    gogT = GT[:, 80:96]
    bsT = GT[:, 96:112]
    kg = GT[:, 113:114]
    qgs = V(sm[6], F32, 1)
    sinke = V(sm[8], F32, 16)
    edge = V(sm[10], F32, 2)
    epsc = V(sm[11], F32, 1)
    st_ss = V(sm[12], F32, 1)
    st_sq = V(sm[13], F32, 1)
    st_rs = V(sm[14], F32, 1)
    st_s4 = V(sm[15], F32, 4)
    st_q4 = V(sm[16], F32, 4)
    st_r4 = V(sm[17], F32, 4)
    st_den = V(sm[18], F32, 2)
    st_rden = V(sm[19], F32, 2)
    st_ssg = V(sm[20], F32, 16)
    st_ssgo = V(sm[21], F32, 16)
    st_g4 = V(sm[22], F32, 4)
    st_gq = V(sm[23], F32, 4)
    st_gr = V(sm[24], F32, 4)

    E = V(A_E, F32, 3, 16, 128)
    qT = V(A_QT, BF16, 16, TT)
    kT = V(A_KT, BF16, 4, 768)
    vS = V(A_V, BF16, 6, 4, 132)
    PT = [V(A_PT, BF16, 3, 512), V(A_PT + 3072, BF16, 3, 512)]
    gv = V(A_GV, BF16, TB, 2048)
    gb = V(A_GB, BF16, TB, 2048)
    vgb = V(A_VG, F32, 2048)
    xs = [V(B_XS0, F32, D), V(A_QT, F32, D), V(A_KT, F32, D)]
    halo = V(B_HALO, BF16, 32, 256)
    aP = V(B_AP, F32, 2048)
    hid = [V(OFF_MIXT, BF16, 8, TT), V(OFF_MIXT + 8192, BF16, 8, TT)]

    savK = [V(M_SAVE + i * 1024, BF16, 4, 128) for i in range(2)]
    savV = [V(M_SAVE + 2048 + i * 1024, BF16, 4, 128) for i in range(2)]
    S = Sched()
    bank_ctr = [0]

    def bank():
        b = bank_ctr[0] % 8
        bank_ctr[0] += 1
        return psum[:, b * 512:(b + 1) * 512]

    def mm(out, lhsT, rhs, start, stop):
        return S.add("pe", lambda e: e.matmul(out, lhsT, rhs, start=start, stop=stop), reads=[lhsT, rhs], writes=[out])

    def tr(out, in_, idm):
        return S.add("pe", lambda e: e.transpose(out, in_, idm), reads=[in_, idm], writes=[out])

    def act(out, in_, func, scale=None, bias=None, accum=None):
        kw = {}
        rd = [in_]
        wr = [out]
        if scale is not None:
            kw["scale"] = scale
            if not isinstance(scale, float):
                rd.append(scale)
        if bias is not None:
            kw["bias"] = bias
            if not isinstance(bias, float):
                rd.append(bias)
        if accum is not None:
            kw["accum_out"] = accum
            wr.append(accum)
        return S.add("act", lambda e: e.activation(out, in_, func, **kw), reads=rd, writes=wr)

    def tt(out, in0, in1, op, eng="dve"):
        return S.add(eng, lambda e: e.tensor_tensor(out, in0, in1, op), reads=[in0, in1], writes=[out])

    def ts(out, in0, s1, s2, op0, op1=None, eng="dve"):
        rd = [in0] + [s for s in (s1, s2) if s is not None and not isinstance(s, float)]
        if op1 is None:
            return S.add(eng, lambda e: e.tensor_scalar(out, in0, s1, s2, op0), reads=rd, writes=[out])
        return S.add(eng, lambda e: e.tensor_scalar(out, in0, s1, s2, op0, op1), reads=rd, writes=[out])

    def stt(out, in0, scalar, in1, op0, op1, eng="dve"):
        rd = [in0, in1] + ([] if isinstance(scalar, float) else [scalar])
        return S.add(eng, lambda e: e.scalar_tensor_tensor(out, in0, scalar, in1, op0, op1), reads=rd, writes=[out])

    def red(out, in_, eng="dve"):
        return S.add(eng, lambda e: e.tensor_reduce(out, in_, AX.X, ALU.add), reads=[in_], writes=[out])

    def recip(out, in_):
        return S.add("dve", lambda e: e.reciprocal(out, in_), reads=[in_], writes=[out])

    def cp(out, in_, eng="dve"):
        return S.add(eng, lambda e: e.tensor_copy(out, in_), reads=[in_], writes=[out])

    def mset(out, val, eng="dve"):
        return S.add(eng, lambda e: e.memset(out, val), reads=[], writes=[out])

    sp_ctr = [0]

    def dma_sp(out, in_, after=(), slow=False):
        ch = "sp%d" % (sp_ctr[0] % 8)
        sp_ctr[0] += 1
        if slow:
            fn = lambda e: e.dma_start(out=out, in_=in_, allow_slow_non_contiguous=True)
        else:
            fn = lambda e: e.dma_start(out=out, in_=in_)
        return S.add("sp", fn, reads=[in_], writes=[out], chan=ch, after=after)

    slot_ctr = [0]

    def load_slot(src, shape):
        i = slot_ctr[0] % NSLOT
        slot_ctr[0] += 1
        view = V(slots[i], BF16, *shape)
        S.add("pool", lambda e: e.dma_start(out=view, in_=src), reads=[], writes=[view], chan="w%d" % i)
        return view

    def bcast(ap, shape):
        return ap.unsqueeze(2).to_broadcast(shape)

    def rstd_from(ss, tmp, out, n, inv_n):
        act(tmp, ss, AF.Sqrt, scale=float(inv_n), bias=epsc[:, 0:1])
        recip(out, tmp)

    S.add("pool", lambda e: e.memset(ident, 0.0), writes=[ident])
    S.add("pool", lambda e: e.affine_select(out=ident, in_=ident, pattern=[[-1, 128]], compare_op=ALU.not_equal,
                                            fill=1.0, base=0, channel_multiplier=1), reads=[ident], writes=[ident])
    S.add("pool", lambda e: e.memset(Jm, 0.0), writes=[Jm])
    S.add("pool", lambda e: e.affine_select(out=Jm, in_=Jm, pattern=[[1, 128]], compare_op=ALU.not_equal,
                                            fill=1.0, base=-127, channel_multiplier=1), reads=[Jm], writes=[Jm])
    mset(epsc, EPS)
    dma_sp(GT, gt_d)
    dma_sp(edge, edge_d)
    dma_sp(sinke, sink_d[0:1, :].to_broadcast([128, 16]))
    ts(qgs, GT[:, 112:113], float(128 ** -0.5), None, ALU.mult)
    act(sinke, sinke, AF.Exp)
    wsl = tmpA.rearrange("p (h s) -> p h s", s=128)
    dma_sp(wsl, ws_d.rearrange("h t s -> t h s"))
    cp(hbf[:, 0:2048], tmpA)
    for g in range(2):
        bk = bank().bitcast(BF16)
        for i in range(8):
            h = g * 8 + i
            tr(bk[:, i * 128:(i + 1) * 128], hbf[:, h * 128:(h + 1) * 128], ident)
        cp(wsT[:, g * 8:(g + 1) * 8, :], bk.rearrange("p (a b) -> p a b", b=128))
    tb_s = tmpA[0:16, 512:1024]
    va_s = tmpA[0:16, 1024:1536]
    te_s = tmpA[0:16, 1536:2048]
    dma_sp(tb_s, tb_d)
    dma_sp(va_s, valid_d)
    act(te_s, tb_s, AF.Exp)
    tt(te_s, te_s, va_s, ALU.mult)
    tsc_wr = dma_sp(tsc, te_s)

    E_src = [bass.AP(tsc_t, 128 * j, [[1, 128], [512, 16], [1, 128]]) for j in range(3)]

    dbg_off = [0]

    def dump(ap_f32_2d, ncols):
        o = dbg_off[0]
        dma_sp(dbg_d[:, o:o + ncols], ap_f32_2d)
        dbg_off[0] += ncols

    st2 = [(V(sm[2], F32, 1), V(sm[3], F32, 1), V(sm[4], F32, 1)), (V(sm[5], F32, 1), V(sm[7], F32, 1), V(sm[9], F32, 1))]

    def norm_blocks(n, src_fn, dst_fns, gT, junk, pre=None, nbuf=2, ss_pre=None):
        def stats(i):
            ss, sq, rs = st2[i % 2]
            if ss_pre is not None:
                red(ss, ss_pre(i).unsqueeze(1))
            else:
                act(junk, src_fn(i), AF.Square, accum=ss)
            act(sq, ss, AF.Sqrt, scale=float(1.0 / D), bias=epsc[:, 0:1])
            recip(rs, sq)

        def apply(i):
            rs = st2[i % 2][2]
            act(hbf, src_fn(i), AF.Copy, scale=rs[:, 0:1])
            for g in range(4):
                bk = bank().bitcast(BF16)
                for k in range(8):
                    c = g * 8 + k
                    tr(bk[:, k * 128:(k + 1) * 128], hbf[:, c * 128:(c + 1) * 128], ident)
                tt(dst_fns[i](g * 8), bk.rearrange("p (a b) -> p a b", b=128), bcast(gT[:, g * 8:(g + 1) * 8], [128, 8, 128]), ALU.mult)

        if pre:
            for i in range(min(nbuf, n)):
                pre(i)
        stats(0)
        for i in range(n):
            if i + 1 < n:
                stats(i + 1)
            apply(i)
            if pre and i + nbuf < n:
                pre(i + nbuf)

    ssx = V(sm[12], F32, 32)
    st16 = V(sm[25], F32, 16)
    sq16 = V(sm[26], F32, 16)
    r16 = V(sm[29], F32, 16)
    hb_rot = [0]

    def qk_group_epi(bks, dest_fn, idm, scale_ap):
        n = len(bks)
        for i in range(n):
            act(tmpA[:, i * 512:(i + 1) * 512], bks[i], AF.Square)
        red(st16[:, 0:4 * n], tmpA[:, 0:n * 512].rearrange("p (a d) -> p a d", d=128))
        act(sq16[:, 0:4 * n], st16[:, 0:4 * n], AF.Sqrt, scale=float(1.0 / 128), bias=epsc[:, 0:1])
        recip(r16[:, 0:4 * n], sq16[:, 0:4 * n])
        hbs = []
        for i in range(n):
            k_ = hb_rot[0] % 8
            hb_rot[0] += 1
            hb_ = hbf[:, k_ * 512:(k_ + 1) * 512]
            hbs.append(hb_)
            tt(hb_.rearrange("p (a b) -> p a b", b=128), bks[i].rearrange("p (a b) -> p a b", b=128),
               bcast(r16[:, 4 * i:4 * i + 4], [128, 4, 128]), ALU.mult)
        pbs = []
        for i in range(n):
            pb = bank().bitcast(BF16)
            for h_ in range(4):
                tr(pb[:, h_ * 128:(h_ + 1) * 128], hbs[i][:, h_ * 128:(h_ + 1) * 128], idm)
            pbs.append(pb)
        for i in range(n):
            act(dest_fn(i), pbs[i][:, 0:512].rearrange("p (a b) -> p a b", b=128), AF.Copy, scale=scale_ap)

    def gemm_tm(lhs_fn, nblk, w_dram, kchunks, col0s, epilogue, group_epi=None):
        for ci, c0 in enumerate(col0s):
            banks = [bank() for _ in range(nblk)]
            nks = kchunks // 8
            for ks in range(nks):
                sl = load_slot(w_dram[ks * 1024:(ks + 1) * 1024, c0:c0 + 512].rearrange("(c p) e -> p c e", p=128), (8, 512))
                for b in range(nblk):
                    for dc in range(8):
                        kc = ks * 8 + dc
                        mm(banks[b], lhs_fn(kc, b), sl[:, dc, :], kc == 0, kc == kchunks - 1)
            if group_epi is not None and group_epi(ci, banks):
                continue
            for b in range(nblk):
                epilogue(ci, b, banks[b])

    last_stores = []
    ntile = NTILE if not debug else 1
    for t in range(ntile):
        row0 = t * TT
        dsts = []
        for bi in range(6):
            if 1 <= bi <= 4:
                dsts.append(lambda c0, bi=bi: hT[:, c0:c0 + 8, (bi - 1) * 128: bi * 128])
            else:
                hb = 0 if bi == 0 else 1
                dsts.append(lambda c0, hb=hb: halo[:, c0:c0 + 8, hb * 128:(hb + 1) * 128])
        carry = (t > 0) and not debug
        if not carry:
            norm_blocks(6, lambda i: xs[i % 3], dsts, g1T, V(A_E, BF16, D),
                        pre=lambda i: dma_sp(xs[i % 3], xp[row0 + i * 128: row0 + (i + 1) * 128, :]), nbuf=3)
        else:
            norm_blocks(5, lambda i: xs[i % 3], dsts[1:], g1T, V(A_E, BF16, D),
                        pre=lambda i: dma_sp(xs[i % 3], xp[row0 + (i + 1) * 128: row0 + (i + 2) * 128, :]), nbuf=3)
        if debug == "P1":
            break

        rot = [0]
        st4 = [(st_s4, st_q4, st_r4), (st_g4, st_gq, st_gr)]

        def lhs6(kc, b):
            if b == 0:
                return halo[:, kc, 0:128]
            if b == 5:
                return halo[:, kc, 128:256]
            return hT[:, kc, (b - 1) * 128: b * 128]

        def epi_kv(ci, b, bk):
            if ci == 0:
                k_ = rot[0]
                rot[0] += 1
                sq = tmpA[:, (k_ % 4) * 512:(k_ % 4) * 512 + 512]
                s4, q4, r4 = st4[k_ % 2]
                hb_ = hbf[:, (k_ % 8) * 512:(k_ % 8) * 512 + 512]
                act(sq, bk, AF.Square)
                red(s4, sq.rearrange("p (a b) -> p a b", b=128))
                rstd_from(s4, q4, r4, 4, 1.0 / 128)
                tt(hb_.rearrange("p (a b) -> p a b", b=128), bk.rearrange("p (a b) -> p a b", b=128),
                   bcast(r4, [128, 4, 128]), ALU.mult)
                pb = bank().bitcast(BF16)
                for i in range(4):
                    tr(pb[:, i * 128:(i + 1) * 128], hb_[:, i * 128:(i + 1) * 128], ident)
                act(kT[:, :, b * 128:(b + 1) * 128], pb[:, 0:512].rearrange("p (a b) -> p a b", b=128), AF.Copy, scale=kg[:, 0:1])
            else:
                cp(vS[:, kvb[b], :, 0:128], bk.rearrange("p (a b) -> p a b", b=128))

        for j in range(3):
            dma_sp(E[:, j, :, :], E_src[j], after=[tsc_wr])
        mset(vS[:, :, :, 128:129], 1.0)
        kvb = [2, 3, 4, 5] if carry else [0, 1, 2, 3, 4, 5]
        if carry:
            cp(kT[:, :, 0:128], savK[0])
            cp(kT[:, :, 128:256], savK[1])
            cp(vS[:, 0, :, 0:128], savV[0])
            cp(vS[:, 1, :, 0:128], savV[1])

        def grp_kv(ci, bks):
            if ci != 0:
                return False
            batches = [(0, 4)] if carry else [(0, 3), (3, 6)]
            for b0, b1 in batches:
                qk_group_epi(bks[b0:b1], (lambda i, b0=b0: kT[:, :, kvb[b0 + i] * 128:(kvb[b0 + i] + 1) * 128]), ident, kg[:, 0:1])
            return True

        gemm_tm((lambda kc, j: lhs6(kc, kvb[j])), len(kvb), w_in_d, 32, [O_K, O_V], epi_kv, group_epi=grp_kv)
        if t < NTILE - 1 and not debug:
            cp(savK[0], kT[:, :, 4 * 128:5 * 128])
            cp(savK[1], kT[:, :, 5 * 128:6 * 128])
            cp(savV[0], vS[:, 4, :, 0:128])
            cp(savV[1], vS[:, 5, :, 0:128])

        def lhs4(kc, b):
            return hT[:, kc, b * 128:(b + 1) * 128]

        def epi_q(ci, b, bk):
            k_ = rot[0]
            rot[0] += 1
            sq = tmpA[:, (k_ % 4) * 512:(k_ % 4) * 512 + 512]
            s4, q4, r4 = st4[k_ % 2]
            hb_ = hbf[:, (k_ % 8) * 512:(k_ % 8) * 512 + 512]
            act(sq, bk, AF.Square)
            red(s4, sq.rearrange("p (a b) -> p a b", b=128))
            rstd_from(s4, q4, r4, 4, 1.0 / 128)
            tt(hb_.rearrange("p (a b) -> p a b", b=128), bk.rearrange("p (a b) -> p a b", b=128),
               bcast(r4, [128, 4, 128]), ALU.mult)
            pb = bank().bitcast(BF16)
            for i in range(4):
                tr(pb[:, i * 128:(i + 1) * 128], hb_[:, i * 128:(i + 1) * 128], Jm)
            act(qT[:, ci * 4:(ci + 1) * 4, b * 128:(b + 1) * 128], pb[:, 0:512].rearrange("p (a b) -> p a b", b=128),
                AF.Copy, scale=qgs[:, 0:1])

        def grp_q(ci, bks):
            qk_group_epi(bks, (lambda i, ci=ci: qT[:, ci * 4:(ci + 1) * 4, i * 128:(i + 1) * 128]), Jm, qgs[:, 0:1])
            return True

        gemm_tm(lhs4, 4, w_in_d, 32, [0, 512, 1024, 1536], epi_q, group_epi=grp_q)

        def pbank(b):
            return psum[:, b * 512:(b + 1) * 512]

        s_par = [0]

        def issue_S(tb, kvh):
            base = 3 * (s_par[0] % 2)
            s_par[0] += 1
            sbs = [pbank(base + j) for j in range(3)]
            for j in range(3):
                mm(sbs[j], kT[:, kvh, (tb + j) * 128:(tb + j + 1) * 128], qT[:, kvh * 4:(kvh + 1) * 4, tb * 128:(tb + 1) * 128], True, True)
            return sbs

        steps = [(a, b) for a in range(TB) for b in range(4)]
        exb = [tmpA[:, 0:1536].rearrange("p (a b) -> p a b", b=512),
               V(M_HBF, F32, 2048)[:, 0:1536].rearrange("p (a b) -> p a b", b=512)]

        def stage_A(si):
            tb, kvh = steps[si]
            sb = issue_S(tb, kvh)
            ex = exb[si % 2]
            for j in range(3):
                act(ex[:, j, :], sb[j], AF.Exp)
            pt = PT[si % 2]
            for j in range(3):
                tt(pt[:, j, :].rearrange("p (a b) -> p a b", b=128), ex[:, j, :].rearrange("p (a b) -> p a b", b=128),
                   E[:, j, kvh * 4:(kvh + 1) * 4, :], ALU.mult)
            if tb == 0 and t == 0:
                ts(pt[:, 0, :], pt[:, 0, :], edge[:, 0:1], None, ALU.mult)
            if tb == TB - 1 and t == NTILE - 1:
                ts(pt[:, 2, :], pt[:, 2, :], edge[:, 1:2], None, ALU.mult)

        def stage_B(si):
            tb, kvh = steps[si]
            pt = PT[si % 2]
            for half in range(2):
                ob = pbank(6 + half)
                for hl in range(2):
                    hh = half * 2 + hl
                    for j in range(3):
                        mm(ob[:, hl * 132: hl * 132 + 129], pt[:, j, hh * 128:(hh + 1) * 128], vS[:, tb + j, kvh, 0:129], j == 0, j == 2)
                h0 = kvh * 4 + half * 2
                ob2 = ob[:, 0:264].rearrange("p (a b) -> p a b", b=132)
                tt(st_den, ob2[:, :, 128], sinke[:, h0:h0 + 2], ALU.add)
                recip(st_rden, st_den)
                tt(aP[:, h0 * 128:(h0 + 2) * 128].rearrange("p (a b) -> p a b", b=128), ob2[:, :, 0:128],
                   bcast(st_rden, [128, 2, 128]), ALU.mult)
            if kvh != 3:
                return
            act(hbf[:, 0:2048], aP, AF.Square, accum=st_ss)
            rstd_from(st_ss, st_sq, st_rs, 1, 1.0 / 2048)
            act(hbf[:, 0:2048], aP, AF.Copy, scale=st_rs[:, 0:1])
            for g in range(2):
                pb = pbank(6 + g).bitcast(BF16)
                for i in range(8):
                    c = g * 8 + i
                    tr(pb[:, i * 128:(i + 1) * 128], hbf[:, c * 128:(c + 1) * 128], Jm)
                tt(mixT[:, g * 8:(g + 1) * 8, tb * 128:(tb + 1) * 128], pb.rearrange("p (a b) -> p a b", b=128),
                   bcast(aogT[:, g * 8:(g + 1) * 8], [128, 8, 128]), ALU.mult)

        stage_A(0)
        for si in range(len(steps)):
            if si + 1 < len(steps):
                stage_A(si + 1)
            stage_B(si)
        if debug == "P23":
            break

        dma_sp(vgb, vgain_d[0:1, :].to_broadcast([128, 2048]))

        def epi_g(ci, b, bk):
            tm = tmpA[:, (ci % 2) * 512:(ci % 2) * 512 + 512]
            act(tm, bk, AF.Gelu_apprx_tanh)
            act(hbf[:, 0:512], tm, AF.Square, accum=st_ssg[:, b * 4 + ci: b * 4 + ci + 1])
            cp(gv[:, b, ci * 512:(ci + 1) * 512], tm)

        gemm_tm(lhs4, 4, w_in_d, 32, [O_G + i * 512 for i in range(4)], epi_g)
        red(st_g4, st_ssg.rearrange("p (a b) -> p a b", b=4))
        rstd_from(st_g4, st_gq, st_gr, 4, 1.0 / 2048)
        for b in range(TB):
            stt(gv[:, b, :], gv[:, b, :], st_gr[:, b:b + 1], vgb, ALU.mult, ALU.mult)

        def epi_u(ci, b, bk):
            svb = bank()
            for hl in range(4):
                hd = ci * 4 + hl
                mm(svb[:, hl * 128:(hl + 1) * 128], wsT[:, hd, :], gv[:, b, hd * 128:(hd + 1) * 128], True, True)
            k_ = rot[0]
            rot[0] += 1
            tm = tmpA[:, (k_ % 2) * 1024:(k_ % 2) * 1024 + 512]
            tm2 = tmpA[:, (k_ % 2) * 1024 + 512:(k_ % 2) * 1024 + 1024]
            act(tm, bk, AF.Gelu_apprx_tanh)
            tt(tm2.rearrange("p (a b) -> p a b", b=128), svb.rearrange("p (a b) -> p a b", b=128),
               bcast(bsT[:, ci * 4:(ci + 1) * 4], [128, 4, 128]), ALU.add)
            tt(tm, tm, tm2, ALU.mult)
            act(hbf[:, 0:512], tm, AF.Square, accum=st_ssgo[:, b * 4 + ci: b * 4 + ci + 1])
            cp(gb[:, b, ci * 512:(ci + 1) * 512], tm)

        gemm_tm(lhs4, 4, w_in_d, 32, [O_U + i * 512 for i in range(4)], epi_u)
        red(st_g4, st_ssgo.rearrange("p (a b) -> p a b", b=4))
        rstd_from(st_g4, st_gq, st_gr, 4, 1.0 / 2048)
        for b in range(TB):
            act(hbf[:, 0:2048], gb[:, b, :], AF.Copy, scale=st_gr[:, b:b + 1])
            for g in range(2):
                pb = bank().bitcast(BF16)
                for i in range(8):
                    c = g * 8 + i
                    tr(pb[:, i * 128:(i + 1) * 128], hbf[:, c * 128:(c + 1) * 128], ident)
                tt(mixT[:, 16 + g * 8:16 + (g + 1) * 8, b * 128:(b + 1) * 128], pb.rearrange("p (a b) -> p a b", b=128),
                   bcast(gogT[:, g * 8:(g + 1) * 8], [128, 8, 128]), ALU.mult)
        if debug == "P2":
            break

        for b in range(TB):
            dma_sp(x1[:, b, :], xp[row0 + 128 + b * 128: row0 + 256 + b * 128, :])

        def lhs_mix(kc, b):
            return mixT[:, kc, b * 128:(b + 1) * 128]

        def epi_out(ci, b, bk):
            tt(x1[:, b, ci * 512:(ci + 1) * 512], x1[:, b, ci * 512:(ci + 1) * 512], bk, ALU.add)
            act(hbf[:, 0:512], x1[:, b, ci * 512:(ci + 1) * 512], AF.Square, accum=ssx[:, b * 8 + ci: b * 8 + ci + 1])

        gemm_tm(lhs_mix, 4, w_out_d, 32, [i * 512 for i in range(8)], epi_out)
        if debug == "P3":
            break

        norm_blocks(TB, lambda i: x1[:, i, :], [(lambda c0, b=b: hT[:, c0:c0 + 8, b * 128:(b + 1) * 128]) for b in range(TB)],
                    g2T, None, ss_pre=lambda i: ssx[:, i * 8:(i + 1) * 8])

        nfg = 16 if debug != "P5s" else 2
        for fg in range(nfg):
            hd_ = hid[fg % 2]
            for r in range(2):
                f0 = fg * 1024 + r * 512
                bks = [bank() for _ in range(4)]
                for kq in range(4):
                    sl = load_slot(w1_d[kq * 1024:(kq + 1) * 1024, f0:f0 + 512].rearrange("(c p) f -> p c f", p=128), (8, 512))
                    for fl in range(4):
                        for dc in range(8):
                            kc = kq * 8 + dc
                            mm(bks[fl], sl[:, dc, fl * 128:(fl + 1) * 128], hT[:, kc, :], kc == 0, kc == 31)
                for fl in range(4):
                    rl = tmpA[:, fl * 512:(fl + 1) * 512]
                    act(rl, bks[fl], AF.Relu)
                    tt(hd_[:, r * 4 + fl, :], rl, bks[fl], ALU.mult)
            for oc in range(8):
                sl = load_slot(w2_d[fg * 1024:(fg + 1) * 1024, oc * 512:(oc + 1) * 512].rearrange("(c p) e -> p c e", p=128), (8, 512))
                for b in range(TB):
                    bk = bank()
                    for fc in range(8):
                        mm(bk, hd_[:, fc, b * 128:(b + 1) * 128], sl[:, fc, :], fc == 0, fc == 7)
                    tt(x1[:, b, oc * 512:(oc + 1) * 512], x1[:, b, oc * 512:(oc + 1) * 512], bk, ALU.add)
                    if fg == nfg - 1 and not debug:
                        last_stores.append(dma_sp(y_d[(t * TB + b) * 128:(t * TB + b + 1) * 128, oc * 512:(oc + 1) * 512],
                                                  x1[:, b, oc * 512:(oc + 1) * 512]))

    if debug:
        if debug == "P1":
            for i in range(8):
                cp(tmpA, V(OFF_HT + i * 4096, BF16, 2048))
                dump(tmpA, 2048)
        elif debug in ("P23", "P2"):
            for i in range(8):
                cp(tmpA, V(OFF_MIXT + i * 4096, BF16, 2048))
                dump(tmpA, 2048)
        elif debug in ("P3", "P5s", "P5"):
            for b in range(TB):
                dump(x1[:, b, :], 4096)
        last_stores.append(S.ops["sp"][-1])
        mset(tmpA, 0.0)
        last_stores.append(dma_sp(y_d[0:128, 0:2048], tmpA))

    S.add("sp", None, after=last_stores)

    chans = sorted(S.chan_count.keys())
    sem_objs = {}
    import contextlib
    with contextlib.ExitStack() as st:
        for e in ("pe", "act", "dve", "pool"):
            sem_objs[e] = st.enter_context(nc.semaphore("s_" + e))
        chan_sems = {c: st.enter_context(nc.semaphore("c_" + c)) for c in chans}
        block = st.enter_context(nc.Block())
        S.emit(nc, block, sem_objs, chan_sems)
    return nc, S


def _t5_bucket(rel):
    nb = 16
    max_exact = 8
    ret = (rel > 0).astype(np.int32) * nb
    n = np.abs(rel)
    large = max_exact + (np.log(np.maximum(n, 1).astype(np.float32) / max_exact)
                         / math.log(128 / max_exact) * (nb - max_exact)).astype(np.int32)
    large = np.minimum(large, nb - 1)
    return ret + np.where(n < max_exact, n, large)


def _host_consts(inputs):
    i = np.arange(512)
    delta = i - 255
    ok = (np.abs(delta) <= 128) & (i < 511)
    bk = _t5_bucket(delta)
    rb = np.asarray(inputs["rel_bias"], np.float32)
    tb = np.zeros((16, 512), np.float32)
    tb[:, ok] = rb[bk[ok], :].T
    valid = np.broadcast_to(ok.astype(np.float32)[None, :], (16, 512)).copy()
    gt = np.zeros((128, 128), np.float32)
    gt[:, 0:32] = np.asarray(inputs["norm1"], np.float32)[0].reshape(32, 128).T
    gt[:, 32:64] = np.asarray(inputs["norm2"], np.float32)[0].reshape(32, 128).T
    gt[:, 64:80] = np.asarray(inputs["attn_out_gain"], np.float32)[0].reshape(16, 128).T
    gt[:, 80:96] = np.asarray(inputs["gmlp_out_gain"], np.float32)[0].reshape(16, 128).T
    gt[:, 96:112] = np.asarray(inputs["gmlp_b_s"], np.float32)[0].T
    gt[:, 112] = np.asarray(inputs["q_gain"], np.float32)[0]
    gt[:, 113] = np.asarray(inputs["k_gain"], np.float32)[0]
    return tb, valid, gt


_CACHE = {}


def make_in_maps(inputs, cores=range(NCORE)):
    x = np.asarray(inputs["x"], dtype=np.float32)
    tb, valid, gt = _host_consts(inputs)
    shared = {
        "tb": tb, "valid": valid, "gt": gt,
        "w_in": np.ascontiguousarray(inputs["w_in"][0]),
        "attn_sink": np.ascontiguousarray(inputs["attn_sink"][0:1]),
        "gmlp_v_gain": np.ascontiguousarray(inputs["gmlp_v_gain"][0:1]),
        "gmlp_w_s": np.ascontiguousarray(inputs["gmlp_w_s"][0]),
        "w_out": np.ascontiguousarray(inputs["w_out"][0]),
        "w1": np.ascontiguousarray(inputs["w1"][0]),
        "w2": np.ascontiguousarray(inputs["w2"][0]),
    }
    shared = {k: np.asarray(v, dtype=np.float32) for k, v in shared.items()}
    maps = []
    for c in cores:
        b, half = c // 2, c % 2
        xpad = np.zeros((TOK_CORE + 256, D), np.float32)
        xpad[128:128 + TOK_CORE] = x[b, half * TOK_CORE:(half + 1) * TOK_CORE]
        ed = np.zeros((128, 2), np.float32)
        if half == 1:
            xpad[0:128] = x[b, TOK_CORE - 128:TOK_CORE]
            ed[:, 0] = 1.0
        else:
            xpad[128 + TOK_CORE:] = x[b, TOK_CORE:TOK_CORE + 128]
            ed[:, 1] = 1.0
        m = dict(shared)
        m["xp"] = xpad
        m["edge"] = ed
        maps.append(m)
    return maps


def kernel(**inputs):
    if "nc" not in _CACHE:
        _CACHE["nc"] = build_program()[0]
    nc = _CACHE["nc"]
    maps = make_in_maps(inputs)
    res = run_bass_kernel_spmd(nc, maps, core_ids=list(range(NCORE)))
    out = np.empty((4, 4096, D), np.float32)
    for c in range(NCORE):
        b, half = c // 2, c % 2
        out[b, half * TOK_CORE:(half + 1) * TOK_CORE] = res.results[c]["y"]
    return out
```

```python
import math
import os
from functools import reduce

import numpy as np
import concourse.bass as bass
import concourse.mybir as mybir
from concourse.bass_utils import run_bass_kernel_spmd

F32 = mybir.dt.float32
BF16 = mybir.dt.bfloat16
U8 = mybir.dt.uint8
AF = mybir.ActivationFunctionType
ALU = mybir.AluOpType
AX = mybir.AxisListType
DTSZ = {F32: 4, BF16: 2, U8: 1}

D = 4096
NCORE = 8
TOK_CORE = 2048
NTILE = 4
TB = 4
TT = 512
IN_W = 7168
D_FF = 16384
EPS = 1e-6
O_K, O_V, O_U, O_G = 2048, 2560, 3072, 5120

OFF_HT = 0
OFF_MIXT = 32768
OFF_X1 = 65536
OFF_SLOTS = 131072
NSLOT = 6
SLOT_B = 8192
OFF_MISC = OFF_SLOTS + NSLOT * SLOT_B
M_HBF = OFF_MISC
M_TMPA = M_HBF + 8192
M_WST = M_TMPA + 8192
M_SMALL = M_WST + 4096
M_SAVE = M_SMALL + 30 * 256
ARENA = M_SAVE + 4096
A_E = OFF_X1 + 0
A_QT = OFF_X1 + 24576
A_KT = OFF_X1 + 40960
A_V = OFF_X1 + 47104
A_PT = OFF_X1 + 53440
A_GV = OFF_X1 + 0
A_GB = OFF_X1 + 16384
A_VG = OFF_X1 + 32768
B_XS0 = OFF_MIXT
B_HALO = OFF_MIXT + 16384
B_AP = OFF_MIXT + 16384


def _prod(s):
    return reduce(lambda a, b: a * b, s, 1)


class Op:
    __slots__ = ("eng", "fn", "is_dma", "chan", "deps", "signal", "tick", "idx", "dval")

    def __init__(self, eng, fn, is_dma, chan):
        self.eng = eng
        self.fn = fn
        self.is_dma = is_dma
        self.chan = chan
        self.deps = {}
        self.signal = False
        self.tick = 0
        self.dval = 0


class Cell:
    __slots__ = ("w", "dw", "r", "dr")

    def __init__(self):
        self.w = {}
        self.dw = []
        self.r = {}
        self.dr = []


class Sched:
    ENGS = ("pe", "act", "dve", "pool", "sp")

    def __init__(self):
        self.ops = {e: [] for e in self.ENGS}
        self.cells = {}
        self.chan_last = {}
        self.chan_count = {}
        self.nops = 0

    @staticmethod
    def _cells(ap):
        name = ap.tensor.name
        if name == "arena":
            cs = 256
        elif name == "psum":
            cs = 2048
        else:
            return ()
        esz = DTSZ[ap.dtype]
        pstep = tuple(ap.ap)[0][0]
        off = int(ap.offset)
        if pstep > 0:
            off = off % pstep
        lo = off * esz
        hi = lo + esz
        for step, cnt in tuple(ap.ap)[1:]:
            if step >= 0:
                hi += step * (cnt - 1) * esz
            else:
                lo += step * (cnt - 1) * esz
        return [(name, c) for c in range(lo // cs, (hi - 1) // cs + 1)]

    def add(self, eng, fn, reads=(), writes=(), chan=None, after=()):
        is_dma = chan is not None
        op = Op(eng, fn, is_dma, chan)
        deps = op.deps

        def dep(o, raw):
            if o is op:
                return
            if o.is_dma:
                deps[o] = True
                return
            same = (o.eng == eng)
            if same and not is_dma and eng == "pe":
                return
            deps[o] = True

        rcells = []
        for ap in reads:
            rcells.extend(self._cells(ap))
        wcells = []
        for ap in writes:
            wcells.extend(self._cells(ap))
        for key in rcells:
            st = self.cells.get(key)
            if st is None:
                st = self.cells[key] = Cell()
            for o in st.w.values():
                dep(o, True)
            for o in st.dw:
                dep(o, True)
        for key in wcells:
            st = self.cells.get(key)
            if st is None:
                st = self.cells[key] = Cell()
            for o in st.r.values():
                dep(o, False)
            for o in st.dr:
                dep(o, False)
            for o in st.w.values():
                dep(o, False)
            for o in st.dw:
                dep(o, False)
        for o in after:
            dep(o, True)
        if is_dma:
            prev = self.chan_last.get(chan)
            if prev is not None:
                deps[prev] = True
            self.chan_last[chan] = op
            n = self.chan_count.get(chan, 0) + 1
            self.chan_count[chan] = n
            op.dval = 16 * n
        for o in deps:
            if not o.is_dma:
                o.signal = True
        for key in rcells:
            st = self.cells[key]
            if is_dma:
                st.dr.append(op)
            else:
                st.r[eng] = op
        for key in wcells:
            st = self.cells[key]
            if st.r or st.dr:
                st.r = {}
                st.dr = []
                st.dw = []
            if is_dma:
                st.dw.append(op)
            else:
                st.w[eng] = op
        self.ops[eng].append(op)
        self.nops += 1
        return op

    def emit(self, nc, block, sems, chan_sems):
        for e in self.ENGS:
            t = 0
            for op in self.ops[e]:
                if op.signal and not op.is_dma:
                    t += 1
                    op.tick = t

        def run(eng_name, engine):
            waited = {}
            for op in self.ops[eng_name]:
                need = {}
                for o in op.deps:
                    if o.is_dma:
                        s, v = chan_sems[o.chan], o.dval
                    else:
                        s, v = sems[o.eng], o.tick
                    if waited.get(s, 0) >= v:
                        continue
                    if need.get(s, (None, 0))[1] < v:
                        need[s] = (s, v)
                for s, v in need.values():
                    engine.wait_ge(s, v)
                    waited[s] = v
                if op.fn is None:
                    continue
                ins = op.fn(engine)
                if op.is_dma:
                    ins.then_inc(chan_sems[op.chan], 16)
                elif op.signal:
                    ins.then_inc(sems[op.eng], 1)

        block.tensor(lambda e: run("pe", e))
        block.scalar(lambda e: run("act", e))
        block.vector(lambda e: run("dve", e))
        block.gpsimd(lambda e: run("pool", e))
        block.sync(lambda e: run("sp", e))


def build_program(debug=None):
    nc = bass.Bass("TRN2", target_bir_lowering=False)
    xp = nc.dram_tensor("xp", [TOK_CORE + 256, D], F32, kind="ExternalInput").ap()
    edge_d = nc.dram_tensor("edge", [128, 2], F32, kind="ExternalInput").ap()
    tb_d = nc.dram_tensor("tb", [16, 512], F32, kind="ExternalInput").ap()
    gt_d = nc.dram_tensor("gt", [128, 128], F32, kind="ExternalInput").ap()
    valid_d = nc.dram_tensor("valid", [16, 512], F32, kind="ExternalInput").ap()
    w_in_d = nc.dram_tensor("w_in", [D, IN_W], F32, kind="ExternalInput").ap()
    sink_d = nc.dram_tensor("attn_sink", [1, 16], F32, kind="ExternalInput").ap()
    vgain_d = nc.dram_tensor("gmlp_v_gain", [1, 2048], F32, kind="ExternalInput").ap()
    ws_d = nc.dram_tensor("gmlp_w_s", [16, 128, 128], F32, kind="ExternalInput").ap()
    w_out_d = nc.dram_tensor("w_out", [D, D], F32, kind="ExternalInput").ap()
    w1_d = nc.dram_tensor("w1", [D, D_FF], F32, kind="ExternalInput").ap()
    w2_d = nc.dram_tensor("w2", [D_FF, D], F32, kind="ExternalInput").ap()
    y_d = nc.dram_tensor("y", [TOK_CORE, D], F32, kind="ExternalOutput").ap()
    tsc_t = nc.dram_tensor("tsc", [16, 512], F32)
    tsc = tsc_t.ap()
    dbg_d = None
    if debug:
        dbg_d = nc.dram_tensor("dbg", [128, 16384], F32, kind="ExternalOutput").ap()

    arena = nc.alloc_sbuf_tensor("arena", [128, ARENA], U8)
    psum = nc.alloc_psum_tensor("psum", [128, 4096], F32)

    def V(off, dt, *shape):
        n = _prod(shape) * DTSZ[dt]
        ap = arena[:, off:off + n].bitcast(dt)
        if len(shape) == 2:
            ap = ap.rearrange("p (a b) -> p a b", b=shape[1])
        elif len(shape) == 3:
            ap = ap.rearrange("p (a b c) -> p a b c", b=shape[1], c=shape[2])
        return ap

    hT = V(OFF_HT, BF16, 32, TT)
    mixT = V(OFF_MIXT, BF16, 32, TT)
    x1 = V(OFF_X1, F32, TB, D)
    slots = [OFF_SLOTS + i * SLOT_B for i in range(NSLOT)]
    hbf = V(M_HBF, BF16, D)
    tmpA = V(M_TMPA, F32, 2048)
    wsT = V(M_WST, BF16, 16, 128)
    sm = [M_SMALL + i * 256 for i in range(30)]
    ident = V(sm[0], BF16, 128)
    Jm = V(sm[1], BF16, 128)
    GT = V(sm[27], F32, 128)
    g1T = GT[:, 0:32]
    g2T = GT[:, 32:64]
    aogT = GT[:, 64:80]
    gogT = GT[:, 80:96]
    bsT = GT[:, 96:112]
    kg = GT[:, 113:114]
    qgs = V(sm[6], F32, 1)
    sinke = V(sm[8], F32, 16)
    edge = V(sm[10], F32, 2)
    epsc = V(sm[11], F32, 1)
    st_ss = V(sm[12], F32, 1)
    st_sq = V(sm[13], F32, 1)
    st_rs = V(sm[14], F32, 1)
    st_s4 = V(sm[15], F32, 4)
    st_q4 = V(sm[16], F32, 4)
    st_r4 = V(sm[17], F32, 4)
    st_den = V(sm[18], F32, 2)
    st_rden = V(sm[19], F32, 2)
    st_ssg = V(sm[20], F32, 16)
    st_ssgo = V(sm[21], F32, 16)
    st_g4 = V(sm[22], F32, 4)
    st_gq = V(sm[23], F32, 4)
    st_gr = V(sm[24], F32, 4)

    E = V(A_E, F32, 3, 16, 128)
    qT = V(A_QT, BF16, 16, TT)
    kT = V(A_KT, BF16, 4, 768)
    vS = V(A_V, BF16, 6, 4, 132)
    PT = [V(A_PT, BF16, 3, 512), V(A_PT + 3072, BF16, 3, 512)]
    gv = V(A_GV, BF16, TB, 2048)
    gb = V(A_GB, BF16, TB, 2048)
    vgb = V(A_VG, F32, 2048)
    xs = [V(B_XS0, F32, D), V(A_QT, F32, D), V(A_KT, F32, D)]
    halo = V(B_HALO, BF16, 32, 256)
    aP = V(B_AP, F32, 2048)
    hid = [V(OFF_MIXT, BF16, 8, TT), V(OFF_MIXT + 8192, BF16, 8, TT)]

    savK = [V(M_SAVE + i * 1024, BF16, 4, 128) for i in range(2)]
    savV = [V(M_SAVE + 2048 + i * 1024, BF16, 4, 128) for i in range(2)]
    S = Sched()
    bank_ctr = [0]

    def bank():
        b = bank_ctr[0] % 8
        bank_ctr[0] += 1
        return psum[:, b * 512:(b + 1) * 512]

    def mm(out, lhsT, rhs, start, stop):
        return S.add("pe", lambda e: e.matmul(out, lhsT, rhs, start=start, stop=stop), reads=[lhsT, rhs], writes=[out])

    def tr(out, in_, idm):
        return S.add("pe", lambda e: e.transpose(out, in_, idm), reads=[in_, idm], writes=[out])

    def act(out, in_, func, scale=None, bias=None, accum=None):
        kw = {}
        rd = [in_]
        wr = [out]
        if scale is not None:
            kw["scale"] = scale
            if not isinstance(scale, float):
                rd.append(scale)
        if bias is not None:
            kw["bias"] = bias
            if not isinstance(bias, float):
                rd.append(bias)
        if accum is not None:
            kw["accum_out"] = accum
            wr.append(accum)
        return S.add("act", lambda e: e.activation(out, in_, func, **kw), reads=rd, writes=wr)

    def tt(out, in0, in1, op, eng="dve"):
        return S.add(eng, lambda e: e.tensor_tensor(out, in0, in1, op), reads=[in0, in1], writes=[out])

    def ts(out, in0, s1, s2, op0, op1=None, eng="dve"):
        rd = [in0] + [s for s in (s1, s2) if s is not None and not isinstance(s, float)]
        if op1 is None:
            return S.add(eng, lambda e: e.tensor_scalar(out, in0, s1, s2, op0), reads=rd, writes=[out])
        return S.add(eng, lambda e: e.tensor_scalar(out, in0, s1, s2, op0, op1), reads=rd, writes=[out])

    def stt(out, in0, scalar, in1, op0, op1, eng="dve"):
        rd = [in0, in1] + ([] if isinstance(scalar, float) else [scalar])
        return S.add(eng, lambda e: e.scalar_tensor_tensor(out, in0, scalar, in1, op0, op1), reads=rd, writes=[out])

    def red(out, in_, eng="dve"):
        return S.add(eng, lambda e: e.tensor_reduce(out, in_, AX.X, ALU.add), reads=[in_], writes=[out])

    def recip(out, in_):
        return S.add("dve", lambda e: e.reciprocal(out, in_), reads=[in_], writes=[out])

    def cp(out, in_, eng="dve"):
        return S.add(eng, lambda e: e.tensor_copy(out, in_), reads=[in_], writes=[out])

    def mset(out, val, eng="dve"):
        return S.add(eng, lambda e: e.memset(out, val), reads=[], writes=[out])

    sp_ctr = [0]

    def dma_sp(out, in_, after=(), slow=False):
        ch = "sp%d" % (sp_ctr[0] % 8)
        sp_ctr[0] += 1
        if slow:
            fn = lambda e: e.dma_start(out=out, in_=in_, allow_slow_non_contiguous=True)
        else:
            fn = lambda e: e.dma_start(out=out, in_=in_)
        return S.add("sp", fn, reads=[in_], writes=[out], chan=ch, after=after)

    slot_ctr = [0]

    def load_slot(src, shape):
        i = slot_ctr[0] % NSLOT
        slot_ctr[0] += 1
        view = V(slots[i], BF16, *shape)
        S.add("pool", lambda e: e.dma_start(out=view, in_=src), reads=[], writes=[view], chan="w%d" % i)
        return view

    def bcast(ap, shape):
        return ap.unsqueeze(2).to_broadcast(shape)

    def rstd_from(ss, tmp, out, n, inv_n):
        act(tmp, ss, AF.Sqrt, scale=float(inv_n), bias=epsc[:, 0:1])
        recip(out, tmp)

    S.add("pool", lambda e: e.memset(ident, 0.0), writes=[ident])
    S.add("pool", lambda e: e.affine_select(out=ident, in_=ident, pattern=[[-1, 128]], compare_op=ALU.not_equal,
                                            fill=1.0, base=0, channel_multiplier=1), reads=[ident], writes=[ident])
    S.add("pool", lambda e: e.memset(Jm, 0.0), writes=[Jm])
    S.add("pool", lambda e: e.affine_select(out=Jm, in_=Jm, pattern=[[1, 128]], compare_op=ALU.not_equal,
                                            fill=1.0, base=-127, channel_multiplier=1), reads=[Jm], writes=[Jm])
    mset(epsc, EPS)
    dma_sp(GT, gt_d)
    dma_sp(edge, edge_d)
    dma_sp(sinke, sink_d[0:1, :].to_broadcast([128, 16]))
    ts(qgs, GT[:, 112:113], float(128 ** -0.5), None, ALU.mult)
    act(sinke, sinke, AF.Exp)
    wsl = tmpA.rearrange("p (h s) -> p h s", s=128)
    dma_sp(wsl, ws_d.rearrange("h t s -> t h s"))
    cp(hbf[:, 0:2048], tmpA)
    for g in range(2):
        bk = bank().bitcast(BF16)
        for i in range(8):
            h = g * 8 + i
            tr(bk[:, i * 128:(i + 1) * 128], hbf[:, h * 128:(h + 1) * 128], ident)
        cp(wsT[:, g * 8:(g + 1) * 8, :], bk.rearrange("p (a b) -> p a b", b=128))
    tb_s = tmpA[0:16, 512:1024]
    va_s = tmpA[0:16, 1024:1536]
    te_s = tmpA[0:16, 1536:2048]
    dma_sp(tb_s, tb_d)
    dma_sp(va_s, valid_d)
    act(te_s, tb_s, AF.Exp)
    tt(te_s, te_s, va_s, ALU.mult)
    tsc_wr = dma_sp(tsc, te_s)

    E_src = [bass.AP(tsc_t, 128 * j, [[1, 128], [512, 16], [1, 128]]) for j in range(3)]

    dbg_off = [0]

    def dump(ap_f32_2d, ncols):
        o = dbg_off[0]
        dma_sp(dbg_d[:, o:o + ncols], ap_f32_2d)
        dbg_off[0] += ncols

    st2 = [(V(sm[2], F32, 1), V(sm[3], F32, 1), V(sm[4], F32, 1)), (V(sm[5], F32, 1), V(sm[7], F32, 1), V(sm[9], F32, 1))]

    def norm_blocks(n, src_fn, dst_fns, gT, junk, pre=None, nbuf=2, ss_pre=None):
        def stats(i):
            ss, sq, rs = st2[i % 2]
            if ss_pre is not None:
                red(ss, ss_pre(i).unsqueeze(1))
            else:
                act(junk, src_fn(i), AF.Square, accum=ss)
            act(sq, ss, AF.Sqrt, scale=float(1.0 / D), bias=epsc[:, 0:1])
            recip(rs, sq)

        def apply(i):
            rs = st2[i % 2][2]
            act(hbf, src_fn(i), AF.Copy, scale=rs[:, 0:1])
            for g in range(4):
                bk = bank().bitcast(BF16)
                for k in range(8):
                    c = g * 8 + k
                    tr(bk[:, k * 128:(k + 1) * 128], hbf[:, c * 128:(c + 1) * 128], ident)
                tt(dst_fns[i](g * 8), bk.rearrange("p (a b) -> p a b", b=128), bcast(gT[:, g * 8:(g + 1) * 8], [128, 8, 128]), ALU.mult)

        if pre:
            for i in range(min(nbuf, n)):
                pre(i)
        stats(0)
        for i in range(n):
            if i + 1 < n:
                stats(i + 1)
            apply(i)
            if pre and i + nbuf < n:
                pre(i + nbuf)

    ssx = V(sm[12], F32, 32)
    st16 = V(sm[25], F32, 16)
    sq16 = V(sm[26], F32, 16)
    r16 = V(sm[29], F32, 16)
    hb_rot = [0]

    def qk_group_epi(bks, dest_fn, idm, scale_ap):
        n = len(bks)
        for i in range(n):
            act(tmpA[:, i * 512:(i + 1) * 512], bks[i], AF.Square)
        red(st16[:, 0:4 * n], tmpA[:, 0:n * 512].rearrange("p (a d) -> p a d", d=128))
        act(sq16[:, 0:4 * n], st16[:, 0:4 * n], AF.Sqrt, scale=float(1.0 / 128), bias=epsc[:, 0:1])
        recip(r16[:, 0:4 * n], sq16[:, 0:4 * n])
        hbs = []
        for i in range(n):
            k_ = hb_rot[0] % 8
            hb_rot[0] += 1
            hb_ = hbf[:, k_ * 512:(k_ + 1) * 512]
            hbs.append(hb_)
            tt(hb_.rearrange("p (a b) -> p a b", b=128), bks[i].rearrange("p (a b) -> p a b", b=128),
               bcast(r16[:, 4 * i:4 * i + 4], [128, 4, 128]), ALU.mult)
        pbs = []
        for i in range(n):
            pb = bank().bitcast(BF16)
            for h_ in range(4):
                tr(pb[:, h_ * 128:(h_ + 1) * 128], hbs[i][:, h_ * 128:(h_ + 1) * 128], idm)
            pbs.append(pb)
        for i in range(n):
            act(dest_fn(i), pbs[i][:, 0:512].rearrange("p (a b) -> p a b", b=128), AF.Copy, scale=scale_ap)

    def gemm_tm(lhs_fn, nblk, w_dram, kchunks, col0s, epilogue, group_epi=None):
        for ci, c0 in enumerate(col0s):
            banks = [bank() for _ in range(nblk)]
            nks = kchunks // 8
            for ks in range(nks):
                sl = load_slot(w_dram[ks * 1024:(ks + 1) * 1024, c0:c0 + 512].rearrange("(c p) e -> p c e", p=128), (8, 512))
                for b in range(nblk):
                    for dc in range(8):
                        kc = ks * 8 + dc
                        mm(banks[b], lhs_fn(kc, b), sl[:, dc, :], kc == 0, kc == kchunks - 1)
            if group_epi is not None and group_epi(ci, banks):
                continue
            for b in range(nblk):
                epilogue(ci, b, banks[b])

    last_stores = []
    ntile = NTILE if not debug else 1
    for t in range(ntile):
        row0 = t * TT
        dsts = []
        for bi in range(6):
            if 1 <= bi <= 4:
                dsts.append(lambda c0, bi=bi: hT[:, c0:c0 + 8, (bi - 1) * 128: bi * 128])
            else:
                hb = 0 if bi == 0 else 1
                dsts.append(lambda c0, hb=hb: halo[:, c0:c0 + 8, hb * 128:(hb + 1) * 128])
        carry = (t > 0) and not debug
        if not carry:
            norm_blocks(6, lambda i: xs[i % 3], dsts, g1T, V(A_E, BF16, D),
                        pre=lambda i: dma_sp(xs[i % 3], xp[row0 + i * 128: row0 + (i + 1) * 128, :]), nbuf=3)
        else:
            norm_blocks(5, lambda i: xs[i % 3], dsts[1:], g1T, V(A_E, BF16, D),
                        pre=lambda i: dma_sp(xs[i % 3], xp[row0 + (i + 1) * 128: row0 + (i + 2) * 128, :]), nbuf=3)
        if debug == "P1":
            break

        rot = [0]
        st4 = [(st_s4, st_q4, st_r4), (st_g4, st_gq, st_gr)]

        def lhs6(kc, b):
            if b == 0:
                return halo[:, kc, 0:128]
            if b == 5:
                return halo[:, kc, 128:256]
            return hT[:, kc, (b - 1) * 128: b * 128]

        def epi_kv(ci, b, bk):
            if ci == 0:
                k_ = rot[0]
                rot[0] += 1
                sq = tmpA[:, (k_ % 4) * 512:(k_ % 4) * 512 + 512]
                s4, q4, r4 = st4[k_ % 2]
                hb_ = hbf[:, (k_ % 8) * 512:(k_ % 8) * 512 + 512]
                act(sq, bk, AF.Square)
                red(s4, sq.rearrange("p (a b) -> p a b", b=128))
                rstd_from(s4, q4, r4, 4, 1.0 / 128)
                tt(hb_.rearrange("p (a b) -> p a b", b=128), bk.rearrange("p (a b) -> p a b", b=128),
                   bcast(r4, [128, 4, 128]), ALU.mult)
                pb = bank().bitcast(BF16)
                for i in range(4):
                    tr(pb[:, i * 128:(i + 1) * 128], hb_[:, i * 128:(i + 1) * 128], ident)
                act(kT[:, :, b * 128:(b + 1) * 128], pb[:, 0:512].rearrange("p (a b) -> p a b", b=128), AF.Copy, scale=kg[:, 0:1])
            else:
                cp(vS[:, kvb[b], :, 0:128], bk.rearrange("p (a b) -> p a b", b=128))

        for j in range(3):
            dma_sp(E[:, j, :, :], E_src[j], after=[tsc_wr])
        mset(vS[:, :, :, 128:129], 1.0)
        kvb = [2, 3, 4, 5] if carry else [0, 1, 2, 3, 4, 5]
        if carry:
            cp(kT[:, :, 0:128], savK[0])
            cp(kT[:, :, 128:256], savK[1])
            cp(vS[:, 0, :, 0:128], savV[0])
            cp(vS[:, 1, :, 0:128], savV[1])

        def grp_kv(ci, bks):
            if ci != 0:
                return False
            batches = [(0, 4)] if carry else [(0, 3), (3, 6)]
            for b0, b1 in batches:
                qk_group_epi(bks[b0:b1], (lambda i, b0=b0: kT[:, :, kvb[b0 + i] * 128:(kvb[b0 + i] + 1) * 128]), ident, kg[:, 0:1])
            return True

        gemm_tm((lambda kc, j: lhs6(kc, kvb[j])), len(kvb), w_in_d, 32, [O_K, O_V], epi_kv, group_epi=grp_kv)
        if t < NTILE - 1 and not debug:
            cp(savK[0], kT[:, :, 4 * 128:5 * 128])
            cp(savK[1], kT[:, :, 5 * 128:6 * 128])
            cp(savV[0], vS[:, 4, :, 0:128])
            cp(savV[1], vS[:, 5, :, 0:128])

        def lhs4(kc, b):
            return hT[:, kc, b * 128:(b + 1) * 128]

        def epi_q(ci, b, bk):
            k_ = rot[0]
            rot[0] += 1
            sq = tmpA[:, (k_ % 4) * 512:(k_ % 4) * 512 + 512]
            s4, q4, r4 = st4[k_ % 2]
            hb_ = hbf[:, (k_ % 8) * 512:(k_ % 8) * 512 + 512]
            act(sq, bk, AF.Square)
            red(s4, sq.rearrange("p (a b) -> p a b", b=128))
            rstd_from(s4, q4, r4, 4, 1.0 / 128)
            tt(hb_.rearrange("p (a b) -> p a b", b=128), bk.rearrange("p (a b) -> p a b", b=128),
               bcast(r4, [128, 4, 128]), ALU.mult)
            pb = bank().bitcast(BF16)
            for i in range(4):
                tr(pb[:, i * 128:(i + 1) * 128], hb_[:, i * 128:(i + 1) * 128], Jm)
            act(qT[:, ci * 4:(ci + 1) * 4, b * 128:(b + 1) * 128], pb[:, 0:512].rearrange("p (a b) -> p a b", b=128),
                AF.Copy, scale=qgs[:, 0:1])

        def grp_q(ci, bks):
            qk_group_epi(bks, (lambda i, ci=ci: qT[:, ci * 4:(ci + 1) * 4, i * 128:(i + 1) * 128]), Jm, qgs[:, 0:1])
            return True

        gemm_tm(lhs4, 4, w_in_d, 32, [0, 512, 1024, 1536], epi_q, group_epi=grp_q)

        def pbank(b):
            return psum[:, b * 512:(b + 1) * 512]

        s_par = [0]

        def issue_S(tb, kvh):
            base = 3 * (s_par[0] % 2)
            s_par[0] += 1
            sbs = [pbank(base + j) for j in range(3)]
            for j in range(3):
                mm(sbs[j], kT[:, kvh, (tb + j) * 128:(tb + j + 1) * 128], qT[:, kvh * 4:(kvh + 1) * 4, tb * 128:(tb + 1) * 128], True, True)
            return sbs

        steps = [(a, b) for a in range(TB) for b in range(4)]
        exb = [tmpA[:, 0:1536].rearrange("p (a b) -> p a b", b=512),
               V(M_HBF, F32, 2048)[:, 0:1536].rearrange("p (a b) -> p a b", b=512)]

        def stage_A(si):
            tb, kvh = steps[si]
            sb = issue_S(tb, kvh)
            ex = exb[si % 2]
            for j in range(3):
                act(ex[:, j, :], sb[j], AF.Exp)
            pt = PT[si % 2]
            for j in range(3):
                tt(pt[:, j, :].rearrange("p (a b) -> p a b", b=128), ex[:, j, :].rearrange("p (a b) -> p a b", b=128),
                   E[:, j, kvh * 4:(kvh + 1) * 4, :], ALU.mult)
            if tb == 0 and t == 0:
                ts(pt[:, 0, :], pt[:, 0, :], edge[:, 0:1], None, ALU.mult)
            if tb == TB - 1 and t == NTILE - 1:
                ts(pt[:, 2, :], pt[:, 2, :], edge[:, 1:2], None, ALU.mult)

        def stage_B(si):
            tb, kvh = steps[si]
            pt = PT[si % 2]
            for half in range(2):
                ob = pbank(6 + half)
                for hl in range(2):
                    hh = half * 2 + hl
                    for j in range(3):
                        mm(ob[:, hl * 132: hl * 132 + 129], pt[:, j, hh * 128:(hh + 1) * 128], vS[:, tb + j, kvh, 0:129], j == 0, j == 2)
                h0 = kvh * 4 + half * 2
                ob2 = ob[:, 0:264].rearrange("p (a b) -> p a b", b=132)
                tt(st_den, ob2[:, :, 128], sinke[:, h0:h0 + 2], ALU.add)
                recip(st_rden, st_den)
                for hl in range(2):
                    act(aP[:, (h0 + hl) * 128:(h0 + hl + 1) * 128], ob[:, hl * 132: hl * 132 + 128], AF.Copy,
                        scale=st_rden[:, hl:hl + 1])
            if kvh != 3:
                return
            act(hbf[:, 0:2048], aP, AF.Square, accum=st_ss)
            rstd_from(st_ss, st_sq, st_rs, 1, 1.0 / 2048)
            act(hbf[:, 0:2048], aP, AF.Copy, scale=st_rs[:, 0:1])
            for g in range(2):
                pb = pbank(6 + g).bitcast(BF16)
                for i in range(8):
                    c = g * 8 + i
                    tr(pb[:, i * 128:(i + 1) * 128], hbf[:, c * 128:(c + 1) * 128], Jm)
                tt(mixT[:, g * 8:(g + 1) * 8, tb * 128:(tb + 1) * 128], pb.rearrange("p (a b) -> p a b", b=128),
                   bcast(aogT[:, g * 8:(g + 1) * 8], [128, 8, 128]), ALU.mult)

        stage_A(0)
        for si in range(len(steps)):
            if si + 1 < len(steps):
                stage_A(si + 1)
            stage_B(si)
        if debug == "P23":
            break

        dma_sp(vgb, vgain_d[0:1, :].to_broadcast([128, 2048]))

        def epi_g(ci, b, bk):
            tm = tmpA[:, (ci % 2) * 512:(ci % 2) * 512 + 512]
            act(tm, bk, AF.Gelu_apprx_tanh)
            act(hbf[:, 0:512], tm, AF.Square, accum=st_ssg[:, b * 4 + ci: b * 4 + ci + 1])
            cp(gv[:, b, ci * 512:(ci + 1) * 512], tm)

        gemm_tm(lhs4, 4, w_in_d, 32, [O_G + i * 512 for i in range(4)], epi_g)
        red(st_g4, st_ssg.rearrange("p (a b) -> p a b", b=4))
        rstd_from(st_g4, st_gq, st_gr, 4, 1.0 / 2048)
        for b in range(TB):
            stt(gv[:, b, :], gv[:, b, :], st_gr[:, b:b + 1], vgb, ALU.mult, ALU.mult)

        def epi_u(ci, b, bk):
            svb = bank()
            for hl in range(4):
                hd = ci * 4 + hl
                mm(svb[:, hl * 128:(hl + 1) * 128], wsT[:, hd, :], gv[:, b, hd * 128:(hd + 1) * 128], True, True)
            k_ = rot[0]
            rot[0] += 1
            tm = tmpA[:, (k_ % 2) * 1024:(k_ % 2) * 1024 + 512]
            tm2 = tmpA[:, (k_ % 2) * 1024 + 512:(k_ % 2) * 1024 + 1024]
            act(tm, bk, AF.Gelu_apprx_tanh)
            tt(tm2.rearrange("p (a b) -> p a b", b=128), svb.rearrange("p (a b) -> p a b", b=128),
               bcast(bsT[:, ci * 4:(ci + 1) * 4], [128, 4, 128]), ALU.add)
            tt(tm, tm, tm2, ALU.mult)
            act(hbf[:, 0:512], tm, AF.Square, accum=st_ssgo[:, b * 4 + ci: b * 4 + ci + 1])
            cp(gb[:, b, ci * 512:(ci + 1) * 512], tm)

        gemm_tm(lhs4, 4, w_in_d, 32, [O_U + i * 512 for i in range(4)], epi_u)
        red(st_g4, st_ssgo.rearrange("p (a b) -> p a b", b=4))
        rstd_from(st_g4, st_gq, st_gr, 4, 1.0 / 2048)
        for b in range(TB):
            act(hbf[:, 0:2048], gb[:, b, :], AF.Copy, scale=st_gr[:, b:b + 1])
            for g in range(2):
                pb = bank().bitcast(BF16)
                for i in range(8):
                    c = g * 8 + i
                    tr(pb[:, i * 128:(i + 1) * 128], hbf[:, c * 128:(c + 1) * 128], ident)
                tt(mixT[:, 16 + g * 8:16 + (g + 1) * 8, b * 128:(b + 1) * 128], pb.rearrange("p (a b) -> p a b", b=128),
                   bcast(gogT[:, g * 8:(g + 1) * 8], [128, 8, 128]), ALU.mult)
        if debug == "P2":
            break

        for b in range(TB):
            dma_sp(x1[:, b, :], xp[row0 + 128 + b * 128: row0 + 256 + b * 128, :])

        def lhs_mix(kc, b):
            return mixT[:, kc, b * 128:(b + 1) * 128]

        def epi_out(ci, b, bk):
            tt(x1[:, b, ci * 512:(ci + 1) * 512], x1[:, b, ci * 512:(ci + 1) * 512], bk, ALU.add)
            act(hbf[:, 0:512], x1[:, b, ci * 512:(ci + 1) * 512], AF.Square, accum=ssx[:, b * 8 + ci: b * 8 + ci + 1])

        gemm_tm(lhs_mix, 4, w_out_d, 32, [i * 512 for i in range(8)], epi_out)
        if debug == "P3":
            break

        norm_blocks(TB, lambda i: x1[:, i, :], [(lambda c0, b=b: hT[:, c0:c0 + 8, b * 128:(b + 1) * 128]) for b in range(TB)],
                    g2T, None, ss_pre=lambda i: ssx[:, i * 8:(i + 1) * 8])

        nfg = 16 if debug != "P5s" else 2
        for fg in range(nfg):
            hd_ = hid[fg % 2]
            for r in range(4):
                f0 = fg * 1024 + r * 256
                bks = [bank(), bank()]
                for kh in range(2):
                    sl = load_slot(w1_d[kh * 2048:(kh + 1) * 2048, f0:f0 + 256].rearrange("(c p) f -> p c f", p=128), (16, 256))
                    for fl in range(2):
                        for dc in range(16):
                            kc = kh * 16 + dc
                            mm(bks[fl], sl[:, dc, fl * 128:(fl + 1) * 128], hT[:, kc, :], kc == 0, kc == 31)
                for fl in range(2):
                    rl = tmpA[:, fl * 512:(fl + 1) * 512]
                    act(rl, bks[fl], AF.Relu)
                    tt(hd_[:, r * 2 + fl, :], rl, bks[fl], ALU.mult)
            for oc in range(8):
                sl = load_slot(w2_d[fg * 1024:(fg + 1) * 1024, oc * 512:(oc + 1) * 512].rearrange("(c p) e -> p c e", p=128), (8, 512))
                for b in range(TB):
                    bk = bank()
                    for fc in range(8):
                        mm(bk, hd_[:, fc, b * 128:(b + 1) * 128], sl[:, fc, :], fc == 0, fc == 7)
                    tt(x1[:, b, oc * 512:(oc + 1) * 512], x1[:, b, oc * 512:(oc + 1) * 512], bk, ALU.add)
                    if fg == nfg - 1 and not debug:
                        last_stores.append(dma_sp(y_d[(t * TB + b) * 128:(t * TB + b + 1) * 128, oc * 512:(oc + 1) * 512],
                                                  x1[:, b, oc * 512:(oc + 1) * 512]))

    if debug:
        if debug == "P1":
            for i in range(8):
                cp(tmpA, V(OFF_HT + i * 4096, BF16, 2048))
                dump(tmpA, 2048)
        elif debug in ("P23", "P2"):
            for i in range(8):
                cp(tmpA, V(OFF_MIXT + i * 4096, BF16, 2048))
                dump(tmpA, 2048)
        elif debug in ("P3", "P5s", "P5"):
            for b in range(TB):
                dump(x1[:, b, :], 4096)
        last_stores.append(S.ops["sp"][-1])
        mset(tmpA, 0.0)
        last_stores.append(dma_sp(y_d[0:128, 0:2048], tmpA))

    S.add("sp", None, after=last_stores)

    chans = sorted(S.chan_count.keys())
    sem_objs = {}
    import contextlib
    with contextlib.ExitStack() as st:
        for e in ("pe", "act", "dve", "pool"):
            sem_objs[e] = st.enter_context(nc.semaphore("s_" + e))
        chan_sems = {c: st.enter_context(nc.semaphore("c_" + c)) for c in chans}
        block = st.enter_context(nc.Block())
        S.emit(nc, block, sem_objs, chan_sems)
    return nc, S


def _t5_bucket(rel):
    nb = 16
    max_exact = 8
    ret = (rel > 0).astype(np.int32) * nb
    n = np.abs(rel)
    large = max_exact + (np.log(np.maximum(n, 1).astype(np.float32) / max_exact)
                         / math.log(128 / max_exact) * (nb - max_exact)).astype(np.int32)
    large = np.minimum(large, nb - 1)
    return ret + np.where(n < max_exact, n, large)


def _host_consts(inputs):
    i = np.arange(512)
    delta = i - 255
    ok = (np.abs(delta) <= 128) & (i < 511)
    bk = _t5_bucket(delta)
    rb = np.asarray(inputs["rel_bias"], np.float32)
    tb = np.zeros((16, 512), np.float32)
    tb[:, ok] = rb[bk[ok], :].T
    valid = np.broadcast_to(ok.astype(np.float32)[None, :], (16, 512)).copy()
    gt = np.zeros((128, 128), np.float32)
    gt[:, 0:32] = np.asarray(inputs["norm1"], np.float32)[0].reshape(32, 128).T
    gt[:, 32:64] = np.asarray(inputs["norm2"], np.float32)[0].reshape(32, 128).T
    gt[:, 64:80] = np.asarray(inputs["attn_out_gain"], np.float32)[0].reshape(16, 128).T
    gt[:, 80:96] = np.asarray(inputs["gmlp_out_gain"], np.float32)[0].reshape(16, 128).T
    gt[:, 96:112] = np.asarray(inputs["gmlp_b_s"], np.float32)[0].T
    gt[:, 112] = np.asarray(inputs["q_gain"], np.float32)[0]
    gt[:, 113] = np.asarray(inputs["k_gain"], np.float32)[0]
    return tb, valid, gt


_CACHE = {}


def make_in_maps(inputs, cores=range(NCORE)):
    x = np.asarray(inputs["x"], dtype=np.float32)
    tb, valid, gt = _host_consts(inputs)
    shared = {
        "tb": tb, "valid": valid, "gt": gt,
        "w_in": np.ascontiguousarray(inputs["w_in"][0]),
        "attn_sink": np.ascontiguousarray(inputs["attn_sink"][0:1]),
        "gmlp_v_gain": np.ascontiguousarray(inputs["gmlp_v_gain"][0:1]),
        "gmlp_w_s": np.ascontiguousarray(inputs["gmlp_w_s"][0]),
        "w_out": np.ascontiguousarray(inputs["w_out"][0]),
        "w1": np.ascontiguousarray(inputs["w1"][0]),
        "w2": np.ascontiguousarray(inputs["w2"][0]),
    }
    shared = {k: np.asarray(v, dtype=np.float32) for k, v in shared.items()}
    maps = []
    for c in cores:
        b, half = c // 2, c % 2
        xpad = np.zeros((TOK_CORE + 256, D), np.float32)
        xpad[128:128 + TOK_CORE] = x[b, half * TOK_CORE:(half + 1) * TOK_CORE]
        ed = np.zeros((128, 2), np.float32)
        if half == 1:
            xpad[0:128] = x[b, TOK_CORE - 128:TOK_CORE]
            ed[:, 0] = 1.0
        else:
            xpad[128 + TOK_CORE:] = x[b, TOK_CORE:TOK_CORE + 128]
            ed[:, 1] = 1.0
        m = dict(shared)
        m["xp"] = xpad
        m["edge"] = ed
        maps.append(m)
    return maps


def kernel(**inputs):
    if "nc" not in _CACHE:
        _CACHE["nc"] = build_program()[0]
    nc = _CACHE["nc"]
    maps = make_in_maps(inputs)
    res = run_bass_kernel_spmd(nc, maps, core_ids=list(range(NCORE)))
    out = np.empty((4, 4096, D), np.float32)
    for c in range(NCORE):
        b, half = c // 2, c % 2
        out[b, half * TOK_CORE:(half + 1) * TOK_CORE] = res.results[c]["y"]
    return out
```

```python
import math
import os
from functools import reduce

import numpy as np
import concourse.bass as bass
import concourse.mybir as mybir
from concourse.bass_utils import run_bass_kernel_spmd

F32 = mybir.dt.float32
BF16 = mybir.dt.bfloat16
U8 = mybir.dt.uint8
AF = mybir.ActivationFunctionType
ALU = mybir.AluOpType
AX = mybir.AxisListType
DTSZ = {F32: 4, BF16: 2, U8: 1}

D = 4096
NCORE = 8
TOK_CORE = 2048
NTILE = 4
TB = 4
TT = 512
IN_W = 7168
D_FF = 16384
EPS = 1e-6
O_K, O_V, O_U, O_G = 2048, 2560, 3072, 5120

OFF_HT = 0
OFF_MIXT = 32768
OFF_X1 = 65536
OFF_SLOTS = 131072
NSLOT = 6
SLOT_B = 8192
OFF_MISC = OFF_SLOTS + NSLOT * SLOT_B
M_HBF = OFF_MISC
M_TMPA = M_HBF + 8192
M_WST = M_TMPA + 8192
M_SMALL = M_WST + 4096
M_SAVE = M_SMALL + 30 * 256
ARENA = M_SAVE + 4096
A_E = OFF_X1 + 0
A_QT = OFF_X1 + 24576
A_KT = OFF_X1 + 40960
A_V = OFF_X1 + 47104
A_PT = OFF_X1 + 53440
A_GV = OFF_X1 + 0
A_GB = OFF_X1 + 16384
A_VG = OFF_X1 + 32768
B_XS0 = OFF_MIXT
B_HALO = OFF_MIXT + 16384
B_AP = OFF_MIXT + 16384


def _prod(s):
    return reduce(lambda a, b: a * b, s, 1)


class Op:
    __slots__ = ("eng", "fn", "is_dma", "chan", "deps", "signal", "tick", "idx", "dval")

    def __init__(self, eng, fn, is_dma, chan):
        self.eng = eng
        self.fn = fn
        self.is_dma = is_dma
        self.chan = chan
        self.deps = {}
        self.signal = False
        self.tick = 0
        self.dval = 0


class Cell:
    __slots__ = ("w", "dw", "r", "dr")

    def __init__(self):
        self.w = {}
        self.dw = []
        self.r = {}
        self.dr = []


class Sched:
    ENGS = ("pe", "act", "dve", "pool", "sp")

    def __init__(self):
        self.ops = {e: [] for e in self.ENGS}
        self.cells = {}
        self.chan_last = {}
        self.chan_count = {}
        self.nops = 0

    @staticmethod
    def _cells(ap):
        name = ap.tensor.name
        if name == "arena":
            cs = 256
        elif name == "psum":
            cs = 2048
        else:
            return ()
        esz = DTSZ[ap.dtype]
        pstep = tuple(ap.ap)[0][0]
        off = int(ap.offset)
        if pstep > 0:
            off = off % pstep
        lo = off * esz
        hi = lo + esz
        for step, cnt in tuple(ap.ap)[1:]:
            if step >= 0:
                hi += step * (cnt - 1) * esz
            else:
                lo += step * (cnt - 1) * esz
        return [(name, c) for c in range(lo // cs, (hi - 1) // cs + 1)]

    def add(self, eng, fn, reads=(), writes=(), chan=None, after=()):
        is_dma = chan is not None
        op = Op(eng, fn, is_dma, chan)
        deps = op.deps

        def dep(o, raw):
            if o is op:
                return
            if o.is_dma:
                deps[o] = True
                return
            same = (o.eng == eng)
            if same and not is_dma and eng == "pe":
                return
            deps[o] = True

        rcells = []
        for ap in reads:
            rcells.extend(self._cells(ap))
        wcells = []
        for ap in writes:
            wcells.extend(self._cells(ap))
        for key in rcells:
            st = self.cells.get(key)
            if st is None:
                st = self.cells[key] = Cell()
            for o in st.w.values():
                dep(o, True)
            for o in st.dw:
                dep(o, True)
        for key in wcells:
            st = self.cells.get(key)
            if st is None:
                st = self.cells[key] = Cell()
            for o in st.r.values():
                dep(o, False)
            for o in st.dr:
                dep(o, False)
            for o in st.w.values():
                dep(o, False)
            for o in st.dw:
                dep(o, False)
        for o in after:
            dep(o, True)
        if is_dma:
            prev = self.chan_last.get(chan)
            if prev is not None:
                deps[prev] = True
            self.chan_last[chan] = op
            n = self.chan_count.get(chan, 0) + 1
            self.chan_count[chan] = n
            op.dval = 16 * n
        for o in deps:
            if not o.is_dma:
                o.signal = True
        for key in rcells:
            st = self.cells[key]
            if is_dma:
                st.dr.append(op)
            else:
                st.r[eng] = op
        for key in wcells:
            st = self.cells[key]
            if st.r or st.dr:
                st.r = {}
                st.dr = []
                st.dw = []
            if is_dma:
                st.dw.append(op)
            else:
                st.w[eng] = op
        self.ops[eng].append(op)
        self.nops += 1
        return op

    def emit(self, nc, block, sems, chan_sems):
        for e in self.ENGS:
            t = 0
            for op in self.ops[e]:
                if op.signal and not op.is_dma:
                    t += 1
                    op.tick = t

        def run(eng_name, engine):
            waited = {}
            for op in self.ops[eng_name]:
                need = {}
                for o in op.deps:
                    if o.is_dma:
                        s, v = chan_sems[o.chan], o.dval
                    else:
                        s, v = sems[o.eng], o.tick
                    if waited.get(s, 0) >= v:
                        continue
                    if need.get(s, (None, 0))[1] < v:
                        need[s] = (s, v)
                for s, v in need.values():
                    engine.wait_ge(s, v)
                    waited[s] = v
                if op.fn is None:
                    continue
                ins = op.fn(engine)
                if op.is_dma:
                    ins.then_inc(chan_sems[op.chan], 16)
                elif op.signal:
                    ins.then_inc(sems[op.eng], 1)

        block.tensor(lambda e: run("pe", e))
        block.scalar(lambda e: run("act", e))
        block.vector(lambda e: run("dve", e))
        block.gpsimd(lambda e: run("pool", e))
        block.sync(lambda e: run("sp", e))


def build_program(debug=None):
    nc = bass.Bass("TRN2", target_bir_lowering=False)
    xp = nc.dram_tensor("xp", [TOK_CORE + 256, D], F32, kind="ExternalInput").ap()
    edge_d = nc.dram_tensor("edge", [128, 2], F32, kind="ExternalInput").ap()
    tb_d = nc.dram_tensor("tb", [16, 512], F32, kind="ExternalInput").ap()
    gt_d = nc.dram_tensor("gt", [128, 128], F32, kind="ExternalInput").ap()
    valid_d = nc.dram_tensor("valid", [16, 512], F32, kind="ExternalInput").ap()
    w_in_d = nc.dram_tensor("w_in", [D, IN_W], F32, kind="ExternalInput").ap()
    sink_d = nc.dram_tensor("attn_sink", [1, 16], F32, kind="ExternalInput").ap()
    vgain_d = nc.dram_tensor("gmlp_v_gain", [1, 2048], F32, kind="ExternalInput").ap()
    ws_d = nc.dram_tensor("gmlp_w_s", [16, 128, 128], F32, kind="ExternalInput").ap()
    w_out_d = nc.dram_tensor("w_out", [D, D], F32, kind="ExternalInput").ap()
    w1_d = nc.dram_tensor("w1", [D, D_FF], F32, kind="ExternalInput").ap()
    w2_d = nc.dram_tensor("w2", [D_FF, D], F32, kind="ExternalInput").ap()
    y_d = nc.dram_tensor("y", [TOK_CORE, D], F32, kind="ExternalOutput").ap()
    tsc_t = nc.dram_tensor("tsc", [16, 512], F32)
    tsc = tsc_t.ap()
    dbg_d = None
    if debug:
        dbg_d = nc.dram_tensor("dbg", [128, 16384], F32, kind="ExternalOutput").ap()

    arena = nc.alloc_sbuf_tensor("arena", [128, ARENA], U8)
    psum = nc.alloc_psum_tensor("psum", [128, 4096], F32)

    def V(off, dt, *shape):
        n = _prod(shape) * DTSZ[dt]
        ap = arena[:, off:off + n].bitcast(dt)
        if len(shape) == 2:
            ap = ap.rearrange("p (a b) -> p a b", b=shape[1])
        elif len(shape) == 3:
            ap = ap.rearrange("p (a b c) -> p a b c", b=shape[1], c=shape[2])
        return ap

    hT = V(OFF_HT, BF16, 32, TT)
    mixT = V(OFF_MIXT, BF16, 32, TT)
    x1 = V(OFF_X1, F32, TB, D)
    slots = [OFF_SLOTS + i * SLOT_B for i in range(NSLOT)]
    hbf = V(M_HBF, BF16, D)
    tmpA = V(M_TMPA, F32, 2048)
    wsT = V(M_WST, BF16, 16, 128)
    sm = [M_SMALL + i * 256 for i in range(30)]
    ident = V(sm[0], BF16, 128)
    Jm = V(sm[1], BF16, 128)
    GT = V(sm[27], F32, 128)
    g1T = GT[:, 0:32]
    g2T = GT[:, 32:64]
    aogT = GT[:, 64:80]
    gogT = GT[:, 80:96]
    bsT = GT[:, 96:112]
    kg = GT[:, 113:114]
    qgs = V(sm[6], F32, 1)
    sinke = V(sm[8], F32, 16)
    edge = V(sm[10], F32, 2)
    epsc = V(sm[11], F32, 1)
    st_ss = V(sm[12], F32, 1)
    st_sq = V(sm[13], F32, 1)
    st_rs = V(sm[14], F32, 1)
    st_s4 = V(sm[15], F32, 4)
    st_q4 = V(sm[16], F32, 4)
    st_r4 = V(sm[17], F32, 4)
    st_den = V(sm[18], F32, 2)
    st_rden = V(sm[19], F32, 2)
    st_ssg = V(sm[20], F32, 16)
    st_ssgo = V(sm[21], F32, 16)
    st_g4 = V(sm[22], F32, 4)
    st_gq = V(sm[23], F32, 4)
    st_gr = V(sm[24], F32, 4)

    E = V(A_E, F32, 3, 16, 128)
    qT = V(A_QT, BF16, TB, 16, 128)
    kT = V(A_KT, BF16, 4, 768)
    vS = V(A_V, BF16, 6, 4, 132)
    PT = [V(A_PT, BF16, 3, 512), V(A_PT + 3072, BF16, 3, 512)]
    gv = V(A_GV, BF16, TB, 2048)
    gb = V(A_GB, BF16, TB, 2048)
    vgb = V(A_VG, F32, 2048)
    xs = [V(B_XS0, F32, D), V(A_QT, F32, D), V(A_KT, F32, D)]
    halo = V(B_HALO, BF16, 32, 256)
    aP = V(B_AP, F32, 2048)
    hid = [V(OFF_MIXT, BF16, 8, TT), V(OFF_MIXT + 8192, BF16, 8, TT)]

    savK = [V(M_SAVE + i * 1024, BF16, 4, 128) for i in range(2)]
    savV = [V(M_SAVE + 2048 + i * 1024, BF16, 4, 128) for i in range(2)]
    S = Sched()
    bank_ctr = [0]

    def bank():
        b = bank_ctr[0] % 8
        bank_ctr[0] += 1
        return psum[:, b * 512:(b + 1) * 512]

    def mm(out, lhsT, rhs, start, stop):
        return S.add("pe", lambda e: e.matmul(out, lhsT, rhs, start=start, stop=stop), reads=[lhsT, rhs], writes=[out])

    def tr(out, in_, idm):
        return S.add("pe", lambda e: e.transpose(out, in_, idm), reads=[in_, idm], writes=[out])

    def act(out, in_, func, scale=None, bias=None, accum=None):
        kw = {}
        rd = [in_]
        wr = [out]
        if scale is not None:
            kw["scale"] = scale
            if not isinstance(scale, float):
                rd.append(scale)
        if bias is not None:
            kw["bias"] = bias
            if not isinstance(bias, float):
                rd.append(bias)
        if accum is not None:
            kw["accum_out"] = accum
            wr.append(accum)
        return S.add("act", lambda e: e.activation(out, in_, func, **kw), reads=rd, writes=wr)

    def tt(out, in0, in1, op, eng="dve"):
        return S.add(eng, lambda e: e.tensor_tensor(out, in0, in1, op), reads=[in0, in1], writes=[out])

    def ts(out, in0, s1, s2, op0, op1=None, eng="dve"):
        rd = [in0] + [s for s in (s1, s2) if s is not None and not isinstance(s, float)]
        if op1 is None:
            return S.add(eng, lambda e: e.tensor_scalar(out, in0, s1, s2, op0), reads=rd, writes=[out])
        return S.add(eng, lambda e: e.tensor_scalar(out, in0, s1, s2, op0, op1), reads=rd, writes=[out])

    def stt(out, in0, scalar, in1, op0, op1, eng="dve"):
        rd = [in0, in1] + ([] if isinstance(scalar, float) else [scalar])
        return S.add(eng, lambda e: e.scalar_tensor_tensor(out, in0, scalar, in1, op0, op1), reads=rd, writes=[out])

    def red(out, in_, eng="dve"):
        return S.add(eng, lambda e: e.tensor_reduce(out, in_, AX.X, ALU.add), reads=[in_], writes=[out])

    def recip(out, in_):
        return S.add("dve", lambda e: e.reciprocal(out, in_), reads=[in_], writes=[out])

    def cp(out, in_, eng="dve"):
        return S.add(eng, lambda e: e.tensor_copy(out, in_), reads=[in_], writes=[out])

    def mset(out, val, eng="dve"):
        return S.add(eng, lambda e: e.memset(out, val), reads=[], writes=[out])

    sp_ctr = [0]

    def dma_sp(out, in_, after=(), slow=False):
        ch = "sp%d" % (sp_ctr[0] % 8)
        sp_ctr[0] += 1
        if slow:
            fn = lambda e: e.dma_start(out=out, in_=in_, allow_slow_non_contiguous=True)
        else:
            fn = lambda e: e.dma_start(out=out, in_=in_)
        return S.add("sp", fn, reads=[in_], writes=[out], chan=ch, after=after)

    slot_ctr = [0]

    def load_slot(src, shape):
        i = slot_ctr[0] % NSLOT
        slot_ctr[0] += 1
        view = V(slots[i], BF16, *shape)
        S.add("pool", lambda e: e.dma_start(out=view, in_=src), reads=[], writes=[view], chan="w%d" % i)
        return view

    def bcast(ap, shape):
        return ap.unsqueeze(2).to_broadcast(shape)

    def rstd_from(ss, tmp, out, n, inv_n):
        act(tmp, ss, AF.Sqrt, scale=float(inv_n), bias=epsc[:, 0:1])
        recip(out, tmp)

    S.add("pool", lambda e: e.memset(ident, 0.0), writes=[ident])
    S.add("pool", lambda e: e.affine_select(out=ident, in_=ident, pattern=[[-1, 128]], compare_op=ALU.not_equal,
                                            fill=1.0, base=0, channel_multiplier=1), reads=[ident], writes=[ident])
    S.add("pool", lambda e: e.memset(Jm, 0.0), writes=[Jm])
    S.add("pool", lambda e: e.affine_select(out=Jm, in_=Jm, pattern=[[1, 128]], compare_op=ALU.not_equal,
                                            fill=1.0, base=-127, channel_multiplier=1), reads=[Jm], writes=[Jm])
    mset(epsc, EPS)
    dma_sp(GT, gt_d)
    dma_sp(edge, edge_d)
    dma_sp(sinke, sink_d[0:1, :].to_broadcast([128, 16]))
    ts(qgs, GT[:, 112:113], float(128 ** -0.5), None, ALU.mult)
    act(sinke, sinke, AF.Exp)
    wsl = tmpA.rearrange("p (h s) -> p h s", s=128)
    dma_sp(wsl, ws_d.rearrange("h t s -> t h s"))
    cp(hbf[:, 0:2048], tmpA)
    for g in range(2):
        bk = bank().bitcast(BF16)
        for i in range(8):
            h = g * 8 + i
            tr(bk[:, i * 128:(i + 1) * 128], hbf[:, h * 128:(h + 1) * 128], ident)
        cp(wsT[:, g * 8:(g + 1) * 8, :], bk.rearrange("p (a b) -> p a b", b=128))
    tb_s = tmpA[0:16, 512:1024]
    va_s = tmpA[0:16, 1024:1536]
    te_s = tmpA[0:16, 1536:2048]
    dma_sp(tb_s, tb_d)
    dma_sp(va_s, valid_d)
    act(te_s, tb_s, AF.Exp)
    tt(te_s, te_s, va_s, ALU.mult)
    tsc_wr = dma_sp(tsc, te_s)

    E_src = [bass.AP(tsc_t, 128 * j, [[1, 128], [512, 16], [1, 128]]) for j in range(3)]

    dbg_off = [0]

    def dump(ap_f32_2d, ncols):
        o = dbg_off[0]
        dma_sp(dbg_d[:, o:o + ncols], ap_f32_2d)
        dbg_off[0] += ncols

    st2 = [(V(sm[2], F32, 1), V(sm[3], F32, 1), V(sm[4], F32, 1)), (V(sm[5], F32, 1), V(sm[7], F32, 1), V(sm[9], F32, 1))]

    def norm_blocks(n, src_fn, dst_fns, gT, junk, pre=None, nbuf=2, ss_pre=None):
        def stats(i):
            ss, sq, rs = st2[i % 2]
            if ss_pre is not None:
                red(ss, ss_pre(i).unsqueeze(1))
            else:
                act(junk, src_fn(i), AF.Square, accum=ss)
            act(sq, ss, AF.Sqrt, scale=float(1.0 / D), bias=epsc[:, 0:1])
            recip(rs, sq)

        def apply(i):
            rs = st2[i % 2][2]
            act(hbf, src_fn(i), AF.Copy, scale=rs[:, 0:1])
            for g in range(4):
                bk = bank().bitcast(BF16)
                for k in range(8):
                    c = g * 8 + k
                    tr(bk[:, k * 128:(k + 1) * 128], hbf[:, c * 128:(c + 1) * 128], ident)
                tt(dst_fns[i](g * 8), bk.rearrange("p (a b) -> p a b", b=128), bcast(gT[:, g * 8:(g + 1) * 8], [128, 8, 128]), ALU.mult)

        if pre:
            for i in range(min(nbuf, n)):
                pre(i)
        stats(0)
        for i in range(n):
            if i + 1 < n:
                stats(i + 1)
            apply(i)
            if pre and i + nbuf < n:
                pre(i + nbuf)

    ssx = V(sm[12], F32, 32)
    st16 = V(sm[25], F32, 16)
    sq16 = V(sm[26], F32, 16)
    r16 = V(sm[29], F32, 16)
    hb_rot = [0]

    def qk_group_epi(bks, dest_fn, idm, scale_ap):
        n = len(bks)
        for i in range(n):
            act(tmpA[:, i * 512:(i + 1) * 512], bks[i], AF.Square)
        red(st16[:, 0:4 * n], tmpA[:, 0:n * 512].rearrange("p (a d) -> p a d", d=128))
        act(sq16[:, 0:4 * n], st16[:, 0:4 * n], AF.Sqrt, scale=float(1.0 / 128), bias=epsc[:, 0:1])
        recip(r16[:, 0:4 * n], sq16[:, 0:4 * n])
        hbs = []
        for i in range(n):
            k_ = hb_rot[0] % 8
            hb_rot[0] += 1
            hb_ = hbf[:, k_ * 512:(k_ + 1) * 512]
            hbs.append(hb_)
            tt(hb_.rearrange("p (a b) -> p a b", b=128), bks[i].rearrange("p (a b) -> p a b", b=128),
               bcast(r16[:, 4 * i:4 * i + 4], [128, 4, 128]), ALU.mult)
        pbs = []
        for i in range(n):
            pb = bank().bitcast(BF16)
            for h_ in range(4):
                tr(pb[:, h_ * 128:(h_ + 1) * 128], hbs[i][:, h_ * 128:(h_ + 1) * 128], idm)
            pbs.append(pb)
        for i in range(n):
            act(dest_fn(i), pbs[i][:, 0:512].rearrange("p (a b) -> p a b", b=128), AF.Copy, scale=scale_ap)

    def gemm_tm(lhs_fn, nblk, w_dram, kchunks, col0s, epilogue, group_epi=None):
        for ci, c0 in enumerate(col0s):
            banks = [bank() for _ in range(nblk)]
            nks = kchunks // 8
            for ks in range(nks):
                sl = load_slot(w_dram[ks * 1024:(ks + 1) * 1024, c0:c0 + 512].rearrange("(c p) e -> p c e", p=128), (8, 512))
                for b in range(nblk):
                    for dc in range(8):
                        kc = ks * 8 + dc
                        mm(banks[b], lhs_fn(kc, b), sl[:, dc, :], kc == 0, kc == kchunks - 1)
            if group_epi is not None and group_epi(ci, banks):
                continue
            for b in range(nblk):
                epilogue(ci, b, banks[b])

    last_stores = []
    ntile = NTILE if not debug else 1
    for t in range(ntile):
        row0 = t * TT
        dsts = []
        for bi in range(6):
            if 1 <= bi <= 4:
                dsts.append(lambda c0, bi=bi: hT[:, c0:c0 + 8, (bi - 1) * 128: bi * 128])
            else:
                hb = 0 if bi == 0 else 1
                dsts.append(lambda c0, hb=hb: halo[:, c0:c0 + 8, hb * 128:(hb + 1) * 128])
        carry = (t > 0) and not debug
        if not carry:
            norm_blocks(6, lambda i: xs[i % 3], dsts, g1T, V(A_E, BF16, D),
                        pre=lambda i: dma_sp(xs[i % 3], xp[row0 + i * 128: row0 + (i + 1) * 128, :]), nbuf=3)
        else:
            norm_blocks(5, lambda i: xs[i % 3], dsts[1:], g1T, V(A_E, BF16, D),
                        pre=lambda i: dma_sp(xs[i % 3], xp[row0 + (i + 1) * 128: row0 + (i + 2) * 128, :]), nbuf=3)
        if debug == "P1":
            break

        rot = [0]
        st4 = [(st_s4, st_q4, st_r4), (st_g4, st_gq, st_gr)]

        def lhs6(kc, b):
            if b == 0:
                return halo[:, kc, 0:128]
            if b == 5:
                return halo[:, kc, 128:256]
            return hT[:, kc, (b - 1) * 128: b * 128]

        def epi_kv(ci, b, bk):
            if ci == 0:
                k_ = rot[0]
                rot[0] += 1
                sq = tmpA[:, (k_ % 4) * 512:(k_ % 4) * 512 + 512]
                s4, q4, r4 = st4[k_ % 2]
                hb_ = hbf[:, (k_ % 8) * 512:(k_ % 8) * 512 + 512]
                act(sq, bk, AF.Square)
                red(s4, sq.rearrange("p (a b) -> p a b", b=128))
                rstd_from(s4, q4, r4, 4, 1.0 / 128)
                tt(hb_.rearrange("p (a b) -> p a b", b=128), bk.rearrange("p (a b) -> p a b", b=128),
                   bcast(r4, [128, 4, 128]), ALU.mult)
                pb = bank().bitcast(BF16)
                for i in range(4):
                    tr(pb[:, i * 128:(i + 1) * 128], hb_[:, i * 128:(i + 1) * 128], ident)
                act(kT[:, :, b * 128:(b + 1) * 128], pb[:, 0:512].rearrange("p (a b) -> p a b", b=128), AF.Copy, scale=kg[:, 0:1])
            else:
                cp(vS[:, kvb[b], :, 0:128], bk.rearrange("p (a b) -> p a b", b=128))

        for j in range(3):
            dma_sp(E[:, j, :, :], E_src[j], after=[tsc_wr])
        mset(vS[:, :, :, 128:129], 1.0)
        kvb = [2, 3, 4, 5] if carry else [0, 1, 2, 3, 4, 5]
        if carry:
            cp(kT[:, :, 0:128], savK[0])
            cp(kT[:, :, 128:256], savK[1])
            cp(vS[:, 0, :, 0:128], savV[0])
            cp(vS[:, 1, :, 0:128], savV[1])

        def grp_kv(ci, bks):
            if ci != 0:
                return False
            batches = [(0, 4)] if carry else [(0, 3), (3, 6)]
            for b0, b1 in batches:
                qk_group_epi(bks[b0:b1], (lambda i, b0=b0: kT[:, :, kvb[b0 + i] * 128:(kvb[b0 + i] + 1) * 128]), ident, kg[:, 0:1])
            return True

        gemm_tm((lambda kc, j: lhs6(kc, kvb[j])), len(kvb), w_in_d, 32, [O_K, O_V], epi_kv, group_epi=grp_kv)
        if t < NTILE - 1 and not debug:
            cp(savK[0], kT[:, :, 4 * 128:5 * 128])
            cp(savK[1], kT[:, :, 5 * 128:6 * 128])
            cp(savV[0], vS[:, 4, :, 0:128])
            cp(savV[1], vS[:, 5, :, 0:128])

        def lhs4(kc, b):
            return hT[:, kc, b * 128:(b + 1) * 128]

        def epi_q(ci, b, bk):
            k_ = rot[0]
            rot[0] += 1
            sq = tmpA[:, (k_ % 4) * 512:(k_ % 4) * 512 + 512]
            s4, q4, r4 = st4[k_ % 2]
            hb_ = hbf[:, (k_ % 8) * 512:(k_ % 8) * 512 + 512]
            act(sq, bk, AF.Square)
            red(s4, sq.rearrange("p (a b) -> p a b", b=128))
            rstd_from(s4, q4, r4, 4, 1.0 / 128)
            tt(hb_.rearrange("p (a b) -> p a b", b=128), bk.rearrange("p (a b) -> p a b", b=128),
               bcast(r4, [128, 4, 128]), ALU.mult)
            pb = bank().bitcast(BF16)
            for i in range(4):
                tr(pb[:, i * 128:(i + 1) * 128], hb_[:, i * 128:(i + 1) * 128], Jm)
            act(qT[:, b, ci * 4:(ci + 1) * 4, :], pb[:, 0:512].rearrange("p (a b) -> p a b", b=128),
                AF.Copy, scale=qgs[:, 0:1])

        def grp_q(ci, bks):
            qk_group_epi(bks, (lambda i, ci=ci: qT[:, i, ci * 4:(ci + 1) * 4, :]), Jm, qgs[:, 0:1])
            return True

        gemm_tm(lhs4, 4, w_in_d, 32, [0, 512, 1024, 1536], epi_q, group_epi=grp_q)

        def pbank(b):
            return psum[:, b * 512:(b + 1) * 512]

        s_par = [0]

        def issue_S(tb, kvh):
            base = 3 * (s_par[0] % 2)
            s_par[0] += 1
            sbs = [pbank(base + j) for j in range(3)]
            for j in range(3):
                mm(sbs[j], kT[:, kvh, (tb + j) * 128:(tb + j + 1) * 128], qT[:, tb, kvh * 4:(kvh + 1) * 4, :], True, True)
            return sbs

        steps = [(a, b) for a in range(TB) for b in range(4)]
        exb = [tmpA[:, 0:1536].rearrange("p (a b) -> p a b", b=512),
               V(M_HBF, F32, 2048)[:, 0:1536].rearrange("p (a b) -> p a b", b=512)]

        def stage_A(si):
            tb, kvh = steps[si]
            sb = issue_S(tb, kvh)
            ex = exb[si % 2]
            for j in range(3):
                act(ex[:, j, :], sb[j], AF.Exp)
            pt = PT[si % 2]
            for j in range(3):
                tt(pt[:, j, :].rearrange("p (a b) -> p a b", b=128), ex[:, j, :].rearrange("p (a b) -> p a b", b=128),
                   E[:, j, kvh * 4:(kvh + 1) * 4, :], ALU.mult)
            if tb == 0 and t == 0:
                ts(pt[:, 0, :], pt[:, 0, :], edge[:, 0:1], None, ALU.mult)
            if tb == TB - 1 and t == NTILE - 1:
                ts(pt[:, 2, :], pt[:, 2, :], edge[:, 1:2], None, ALU.mult)

        def stage_B(si):
            tb, kvh = steps[si]
            pt = PT[si % 2]
            for half in range(2):
                ob = pbank(6 + half)
                for hl in range(2):
                    hh = half * 2 + hl
                    for j in range(3):
                        mm(ob[:, hl * 132: hl * 132 + 129], pt[:, j, hh * 128:(hh + 1) * 128], vS[:, tb + j, kvh, 0:129], j == 0, j == 2)
                h0 = kvh * 4 + half * 2
                ob2 = ob[:, 0:264].rearrange("p (a b) -> p a b", b=132)
                tt(st_den, ob2[:, :, 128], sinke[:, h0:h0 + 2], ALU.add)
                recip(st_rden, st_den)
                for hl in range(2):
                    act(aP[:, (h0 + hl) * 128:(h0 + hl + 1) * 128], ob[:, hl * 132: hl * 132 + 128], AF.Copy,
                        scale=st_rden[:, hl:hl + 1])
            if kvh != 3:
                return
            act(hbf[:, 0:2048], aP, AF.Square, accum=st_ss)
            rstd_from(st_ss, st_sq, st_rs, 1, 1.0 / 2048)
            act(hbf[:, 0:2048], aP, AF.Copy, scale=st_rs[:, 0:1])
            for g in range(2):
                pb = pbank(6 + g).bitcast(BF16)
                for i in range(8):
                    c = g * 8 + i
                    tr(pb[:, i * 128:(i + 1) * 128], hbf[:, c * 128:(c + 1) * 128], Jm)
                tt(mixT[:, g * 8:(g + 1) * 8, tb * 128:(tb + 1) * 128], pb.rearrange("p (a b) -> p a b", b=128),
                   bcast(aogT[:, g * 8:(g + 1) * 8], [128, 8, 128]), ALU.mult)

        stage_A(0)
        for si in range(len(steps)):
            if si + 1 < len(steps):
                stage_A(si + 1)
            stage_B(si)
        if debug == "P23":
            break

        dma_sp(vgb, vgain_d[0:1, :].to_broadcast([128, 2048]))

        def epi_g(ci, b, bk):
            tm = tmpA[:, (ci % 2) * 512:(ci % 2) * 512 + 512]
            act(tm, bk, AF.Gelu_apprx_tanh)
            act(hbf[:, 0:512], tm, AF.Square, accum=st_ssg[:, b * 4 + ci: b * 4 + ci + 1])
            cp(gv[:, b, ci * 512:(ci + 1) * 512], tm)

        gemm_tm(lhs4, 4, w_in_d, 32, [O_G + i * 512 for i in range(4)], epi_g)
        red(st_g4, st_ssg.rearrange("p (a b) -> p a b", b=4))
        rstd_from(st_g4, st_gq, st_gr, 4, 1.0 / 2048)
        for b in range(TB):
            stt(gv[:, b, :], gv[:, b, :], st_gr[:, b:b + 1], vgb, ALU.mult, ALU.mult)

        def epi_u(ci, b, bk):
            svb = bank()
            for hl in range(4):
                hd = ci * 4 + hl
                mm(svb[:, hl * 128:(hl + 1) * 128], wsT[:, hd, :], gv[:, b, hd * 128:(hd + 1) * 128], True, True)
            k_ = rot[0]
            rot[0] += 1
            tm = tmpA[:, (k_ % 2) * 1024:(k_ % 2) * 1024 + 512]
            tm2 = tmpA[:, (k_ % 2) * 1024 + 512:(k_ % 2) * 1024 + 1024]
            act(tm, bk, AF.Gelu_apprx_tanh)
            tt(tm2.rearrange("p (a b) -> p a b", b=128), svb.rearrange("p (a b) -> p a b", b=128),
               bcast(bsT[:, ci * 4:(ci + 1) * 4], [128, 4, 128]), ALU.add)
            tt(tm, tm, tm2, ALU.mult)
            act(hbf[:, 0:512], tm, AF.Square, accum=st_ssgo[:, b * 4 + ci: b * 4 + ci + 1])
            cp(gb[:, b, ci * 512:(ci + 1) * 512], tm)

        gemm_tm(lhs4, 4, w_in_d, 32, [O_U + i * 512 for i in range(4)], epi_u)
        red(st_g4, st_ssgo.rearrange("p (a b) -> p a b", b=4))
        rstd_from(st_g4, st_gq, st_gr, 4, 1.0 / 2048)
        for b in range(TB):
            act(hbf[:, 0:2048], gb[:, b, :], AF.Copy, scale=st_gr[:, b:b + 1])
            for g in range(2):
                pb = bank().bitcast(BF16)
                for i in range(8):
                    c = g * 8 + i
                    tr(pb[:, i * 128:(i + 1) * 128], hbf[:, c * 128:(c + 1) * 128], ident)
                tt(mixT[:, 16 + g * 8:16 + (g + 1) * 8, b * 128:(b + 1) * 128], pb.rearrange("p (a b) -> p a b", b=128),
                   bcast(gogT[:, g * 8:(g + 1) * 8], [128, 8, 128]), ALU.mult)
        if debug == "P2":
            break

        for b in range(TB):
            dma_sp(x1[:, b, :], xp[row0 + 128 + b * 128: row0 + 256 + b * 128, :])

        def lhs_mix(kc, b):
            return mixT[:, kc, b * 128:(b + 1) * 128]

        def epi_out(ci, b, bk):
            tt(x1[:, b, ci * 512:(ci + 1) * 512], x1[:, b, ci * 512:(ci + 1) * 512], bk, ALU.add)
            act(hbf[:, 0:512], x1[:, b, ci * 512:(ci + 1) * 512], AF.Square, accum=ssx[:, b * 8 + ci: b * 8 + ci + 1])

        gemm_tm(lhs_mix, 4, w_out_d, 32, [i * 512 for i in range(8)], epi_out)
        if debug == "P3":
            break

        norm_blocks(TB, lambda i: x1[:, i, :], [(lambda c0, b=b: hT[:, c0:c0 + 8, b * 128:(b + 1) * 128]) for b in range(TB)],
                    g2T, None, ss_pre=lambda i: ssx[:, i * 8:(i + 1) * 8])

        nfg = 16 if debug != "P5s" else 2
        for fg in range(nfg):
            hd_ = hid[fg % 2]
            for r in range(4):
                f0 = fg * 1024 + r * 256
                bks = [bank(), bank()]
                for kh in range(2):
                    sl = load_slot(w1_d[kh * 2048:(kh + 1) * 2048, f0:f0 + 256].rearrange("(c p) f -> p c f", p=128), (16, 256))
                    for fl in range(2):
                        for dc in range(16):
                            kc = kh * 16 + dc
                            mm(bks[fl], sl[:, dc, fl * 128:(fl + 1) * 128], hT[:, kc, :], kc == 0, kc == 31)
                for fl in range(2):
                    rl = tmpA[:, fl * 512:(fl + 1) * 512]
                    act(rl, bks[fl], AF.Relu)
                    tt(hd_[:, r * 2 + fl, :], rl, bks[fl], ALU.mult)
            for oc in range(8):
                sl = load_slot(w2_d[fg * 1024:(fg + 1) * 1024, oc * 512:(oc + 1) * 512].rearrange("(c p) e -> p c e", p=128), (8, 512))
                for b in range(TB):
                    bk = bank()
                    for fc in range(8):
                        mm(bk, hd_[:, fc, b * 128:(b + 1) * 128], sl[:, fc, :], fc == 0, fc == 7)
                    tt(x1[:, b, oc * 512:(oc + 1) * 512], x1[:, b, oc * 512:(oc + 1) * 512], bk, ALU.add)
                    if fg == nfg - 1 and not debug:
                        last_stores.append(dma_sp(y_d[(t * TB + b) * 128:(t * TB + b + 1) * 128, oc * 512:(oc + 1) * 512],
                                                  x1[:, b, oc * 512:(oc + 1) * 512]))

    if debug:
        if debug == "P1":
            for i in range(8):
                cp(tmpA, V(OFF_HT + i * 4096, BF16, 2048))
                dump(tmpA, 2048)
        elif debug in ("P23", "P2"):
            for i in range(8):
                cp(tmpA, V(OFF_MIXT + i * 4096, BF16, 2048))
                dump(tmpA, 2048)
        elif debug in ("P3", "P5s", "P5"):
            for b in range(TB):
                dump(x1[:, b, :], 4096)
        last_stores.append(S.ops["sp"][-1])
        mset(tmpA, 0.0)
        last_stores.append(dma_sp(y_d[0:128, 0:2048], tmpA))

    S.add("sp", None, after=last_stores)

    chans = sorted(S.chan_count.keys())
    sem_objs = {}
    import contextlib
    with contextlib.ExitStack() as st:
        for e in ("pe", "act", "dve", "pool"):
            sem_objs[e] = st.enter_context(nc.semaphore("s_" + e))
        chan_sems = {c: st.enter_context(nc.semaphore("c_" + c)) for c in chans}
        block = st.enter_context(nc.Block())
        S.emit(nc, block, sem_objs, chan_sems)
    return nc, S


def _t5_bucket(rel):
    nb = 16
    max_exact = 8
    ret = (rel > 0).astype(np.int32) * nb
    n = np.abs(rel)
    large = max_exact + (np.log(np.maximum(n, 1).astype(np.float32) / max_exact)
                         / math.log(128 / max_exact) * (nb - max_exact)).astype(np.int32)
    large = np.minimum(large, nb - 1)
    return ret + np.where(n < max_exact, n, large)


def _host_consts(inputs):
    i = np.arange(512)
    delta = i - 255
    ok = (np.abs(delta) <= 128) & (i < 511)
    bk = _t5_bucket(delta)
    rb = np.asarray(inputs["rel_bias"], np.float32)
    tb = np.zeros((16, 512), np.float32)
    tb[:, ok] = rb[bk[ok], :].T
    valid = np.broadcast_to(ok.astype(np.float32)[None, :], (16, 512)).copy()
    gt = np.zeros((128, 128), np.float32)
    gt[:, 0:32] = np.asarray(inputs["norm1"], np.float32)[0].reshape(32, 128).T
    gt[:, 32:64] = np.asarray(inputs["norm2"], np.float32)[0].reshape(32, 128).T
    gt[:, 64:80] = np.asarray(inputs["attn_out_gain"], np.float32)[0].reshape(16, 128).T
    gt[:, 80:96] = np.asarray(inputs["gmlp_out_gain"], np.float32)[0].reshape(16, 128).T
    gt[:, 96:112] = np.asarray(inputs["gmlp_b_s"], np.float32)[0].T
    gt[:, 112] = np.asarray(inputs["q_gain"], np.float32)[0]
    gt[:, 113] = np.asarray(inputs["k_gain"], np.float32)[0]
    return tb, valid, gt


_CACHE = {}


def make_in_maps(inputs, cores=range(NCORE)):
    x = np.asarray(inputs["x"], dtype=np.float32)
    tb, valid, gt = _host_consts(inputs)
    shared = {
        "tb": tb, "valid": valid, "gt": gt,
        "w_in": np.ascontiguousarray(inputs["w_in"][0]),
        "attn_sink": np.ascontiguousarray(inputs["attn_sink"][0:1]),
        "gmlp_v_gain": np.ascontiguousarray(inputs["gmlp_v_gain"][0:1]),
        "gmlp_w_s": np.ascontiguousarray(inputs["gmlp_w_s"][0]),
        "w_out": np.ascontiguousarray(inputs["w_out"][0]),
        "w1": np.ascontiguousarray(inputs["w1"][0]),
        "w2": np.ascontiguousarray(inputs["w2"][0]),
    }
    shared = {k: np.asarray(v, dtype=np.float32) for k, v in shared.items()}
    maps = []
    for c in cores:
        b, half = c // 2, c % 2
        xpad = np.zeros((TOK_CORE + 256, D), np.float32)
        xpad[128:128 + TOK_CORE] = x[b, half * TOK_CORE:(half + 1) * TOK_CORE]
        ed = np.zeros((128, 2), np.float32)
        if half == 1:
            xpad[0:128] = x[b, TOK_CORE - 128:TOK_CORE]
            ed[:, 0] = 1.0
        else:
            xpad[128 + TOK_CORE:] = x[b, TOK_CORE:TOK_CORE + 128]
            ed[:, 1] = 1.0
        m = dict(shared)
        m["xp"] = xpad
        m["edge"] = ed
        maps.append(m)
    return maps


def kernel(**inputs):
    if "nc" not in _CACHE:
        _CACHE["nc"] = build_program()[0]
    nc = _CACHE["nc"]
    maps = make_in_maps(inputs)
    res = run_bass_kernel_spmd(nc, maps, core_ids=list(range(NCORE)))
    out = np.empty((4, 4096, D), np.float32)
    for c in range(NCORE):
        b, half = c // 2, c % 2
        out[b, half * TOK_CORE:(half + 1) * TOK_CORE] = res.results[c]["y"]
    return out
```
